# Optimizing a Trainium2 kernel written in Bass

```python
import jax, jax.numpy as jnp
from jax import lax
import numpy as np

D_MODEL = 1024
BATCH = 8
SEQ = 4096
DEPTH = 1

N_META = 16
D_MIX = D_MODEL
D_RG = D_MIX // 2
RG_HEADS = 8
RG_HEAD_DIM = D_RG // RG_HEADS
CONV_W = 4
LRU_C = 8.0
D_HG = D_MIX - D_RG
HG_HEAD_DIM = 128
HG_HEADS = D_HG // HG_HEAD_DIM
HG_CHUNK = 64
D_IN = 2 * D_RG + 4 * D_HG
D_FF = ((8 * D_MODEL // 3 + 255) // 256) * 256
EPS = 1e-6

kernel_name = "hymba_rglru_hgrn2_block"


def rmsnorm(x, g):
    xf = x.astype(jnp.float32)
    y = xf * lax.rsqrt(jnp.mean(xf * xf, axis=-1, keepdims=True) + EPS) * g.astype(jnp.float32)
    return y.astype(x.dtype)


def _lin_combine(e1, e2):
    a1, b1 = e1
    a2, b2 = e2
    return a1 * a2, a2 * b1 + b2


def rg_lru_group(xr, gr, conv_w, conv_b, w_r, b_r, w_i, b_i, lam, norm_g):
    B, L, _ = xr.shape
    xp = jnp.pad(xr.astype(jnp.float32), ((0, 0), (CONV_W - 1, 0), (0, 0)))
    cw = conv_w.astype(jnp.float32)
    xc = conv_b.astype(jnp.float32) + sum(xp[:, j:j + L] * cw[j] for j in range(CONV_W))
    xh = xc.reshape(B, L, RG_HEADS, RG_HEAD_DIM)
    r = jax.nn.sigmoid(jnp.einsum('blhi,hij->blhj', xh, w_r.astype(jnp.float32)).reshape(B, L, D_RG) + b_r.astype(jnp.float32))
    i = jax.nn.sigmoid(jnp.einsum('blhi,hij->blhj', xh, w_i.astype(jnp.float32)).reshape(B, L, D_RG) + b_i.astype(jnp.float32))
    log_a = -LRU_C * jax.nn.softplus(-lam.astype(jnp.float32)) * r
    a = jnp.exp(log_a)
    bx = jnp.sqrt(-jnp.expm1(2.0 * log_a)) * (i * xc)
    _, h = lax.associative_scan(_lin_combine, (a, bx), axis=1)
    y = jax.nn.gelu(gr.astype(jnp.float32)) * h
    return rmsnorm(y, norm_g)


def _to_chunks(t, pad):
    B, L, _ = t.shape
    t = jnp.pad(t, ((0, 0), (pad, 0), (0, 0)))
    n = (L + pad) // HG_CHUNK
    t = t.reshape(B, n, HG_CHUNK, HG_HEADS, HG_HEAD_DIM)
    return jnp.transpose(t, (1, 0, 3, 2, 4))


def _hgrn2_chunk_step(S, inp):
    q, k, v, lf = inp
    b = jnp.cumsum(lf, axis=2)
    inter = jnp.einsum('bhck,bhkv->bhcv', q * jnp.exp(b), S)
    diff = b[:, :, :, None, :] - b[:, :, None, :, :]
    causal = (jnp.arange(HG_CHUNK)[:, None] >= jnp.arange(HG_CHUNK)[None, :])[None, None, :, :, None]
    decay = jnp.where(causal, jnp.exp(jnp.where(causal, diff, 0.0)), 0.0)
    A = jnp.einsum('bhtsk,bhsk->bhts', q[:, :, :, None, :] * decay, k)
    intra = jnp.einsum('bhts,bhsv->bhtv', A, v)
    b_last = b[:, :, -1:, :]
    S_new = jnp.exp(b_last[:, :, 0, :])[..., None] * S + jnp.einsum('bhsk,bhsv->bhkv', k * jnp.exp(b_last - b), v)
    return S_new, inter + intra


def hgrn2_group(hq, hf, hi, hg, lb, norm_g):
    B, L, _ = hq.shape
    lb = lb.astype(jnp.float32)
    q = jax.nn.silu(hq.astype(jnp.float32))
    f = lb + (1.0 - lb) * jax.nn.sigmoid(hf.astype(jnp.float32))
    log_f = jnp.log(f)
    k = 1.0 - f
    v = hi.astype(jnp.float32)
    pad = HG_CHUNK - N_META
    qc, kc, vc, lfc = (_to_chunks(t, pad) for t in (q, k, v, log_f))
    S0 = jnp.zeros((B, HG_HEADS, HG_HEAD_DIM, HG_HEAD_DIM), jnp.float32)
    _, o = lax.scan(_hgrn2_chunk_step, S0, (qc, kc, vc, lfc))
    n = o.shape[0]
    o = jnp.transpose(o, (1, 0, 3, 2, 4)).reshape(B, n * HG_CHUNK, HG_HEADS, HG_HEAD_DIM)[:, pad:]
    o = rmsnorm(o, norm_g).astype(jnp.float32) * jax.nn.silu(hg.astype(jnp.float32).reshape(B, L, HG_HEADS, HG_HEAD_DIM))
    return o.reshape(B, L, D_HG)


def setup_inputs(seed: int = 0) -> dict:
    key = jax.random.key(seed)
    ks = jax.random.split(key, 24)
    f32 = jnp.float32
    nrm = lambda k, shape, s: s * jax.random.normal(k, shape, f32)
    u = jax.random.uniform(ks[9], (DEPTH, D_RG), f32, 0.9, 0.999)
    s = u ** (1.0 / LRU_C)
    lru_lambda = jnp.log(s) - jnp.log1p(-s)
    return {
        "x": jax.random.normal(ks[0], (BATCH, SEQ, D_MODEL), f32),
        "meta_tokens": nrm(ks[1], (N_META, D_MODEL), 1.0),
        "mix_norm_g": 1.0 + nrm(ks[2], (DEPTH, D_MODEL), 0.02),
        "w_in": nrm(ks[3], (DEPTH, D_MODEL, D_IN), D_MODEL ** -0.5),
        "conv_w": nrm(ks[4], (DEPTH, CONV_W, D_RG), CONV_W ** -0.5),
        "conv_b": nrm(ks[5], (DEPTH, D_RG), 0.01),
        "w_rgate": nrm(ks[6], (DEPTH, RG_HEADS, RG_HEAD_DIM, RG_HEAD_DIM), RG_HEAD_DIM ** -0.5),
        "b_rgate": nrm(ks[7], (DEPTH, D_RG), 0.01),
        "w_igate": nrm(ks[8], (DEPTH, RG_HEADS, RG_HEAD_DIM, RG_HEAD_DIM), RG_HEAD_DIM ** -0.5),
        "b_igate": nrm(ks[10], (DEPTH, D_RG), 0.01),
        "lru_lambda": lru_lambda,
        "rg_norm_g": 1.0 + nrm(ks[11], (DEPTH, D_RG), 0.02),
        "hg_lower_bound": nrm(ks[12], (DEPTH + 1, D_HG), 0.1),
        "hg_norm_g": 1.0 + nrm(ks[13], (DEPTH, HG_HEAD_DIM), 0.02),
        "w_out": nrm(ks[14], (DEPTH, D_MIX, D_MODEL), D_MIX ** -0.5),
        "ffn_norm_g": 1.0 + nrm(ks[15], (DEPTH, D_MODEL), 0.02),
        "w_gate_up": nrm(ks[16], (DEPTH, D_MODEL, 2 * D_FF), D_MODEL ** -0.5),
        "w_down": nrm(ks[17], (DEPTH, D_FF, D_MODEL), D_FF ** -0.5),
        "final_norm_g": 1.0 + nrm(ks[18], (D_MODEL,), 0.02),
    }


def reference(x, meta_tokens, mix_norm_g, w_in, conv_w, conv_b, w_rgate, b_rgate, w_igate, b_igate,
              lru_lambda, rg_norm_g, hg_lower_bound, hg_norm_g, w_out, ffn_norm_g, w_gate_up, w_down,
              final_norm_g):
    B = x.shape[0]
    meta = jnp.broadcast_to(meta_tokens.astype(x.dtype)[None], (B, N_META, D_MODEL))
    h = jnp.concatenate([meta, x], axis=1)
    lbs = jnp.cumsum(jax.nn.softmax(hg_lower_bound.astype(jnp.float32), axis=0), axis=0)
    splits = np.cumsum([D_RG, D_RG, D_HG, D_HG, D_HG])
    for l in range(DEPTH):
        u = rmsnorm(h, mix_norm_g[l])
        p = jnp.einsum('bld,de->ble', u, w_in[l])
        rg_x, rg_g, hq, hf, hi, hg = jnp.split(p, splits, axis=-1)
        y_rg = rg_lru_group(rg_x, rg_g, conv_w[l], conv_b[l], w_rgate[l], b_rgate[l],
                            w_igate[l], b_igate[l], lru_lambda[l], rg_norm_g[l])
        y_hg = hgrn2_group(hq, hf, hi, hg, lbs[l], hg_norm_g[l])
        y = jnp.concatenate([y_rg.astype(h.dtype), y_hg.astype(h.dtype)], axis=-1)
        h = h + jnp.einsum('ble,ed->bld', y, w_out[l])
        v = rmsnorm(h, ffn_norm_g[l])
        gate, up = jnp.split(jnp.einsum('bld,df->blf', v, w_gate_up[l]), 2, axis=-1)
        h = h + jnp.einsum('blf,fd->bld', jax.nn.silu(gate) * up, w_down[l])
    return rmsnorm(h, final_norm_g)[:, N_META:]
```

```python
from contextlib import ExitStack
import numpy as np
import concourse.bass as bass
import concourse.mybir as mybir
from concourse.bass_utils import run_bass_kernel_spmd

F32 = mybir.dt.float32
BF16 = mybir.dt.bfloat16
AF = mybir.ActivationFunctionType
ALU = mybir.AluOpType

D = 1024
KC = 8
SEQ = 4096
NMETA = 16
T = SEQ + NMETA
DRG = 512
DHG = 512
DIN = 3072
DFF = 2816
NJ = DFF // 128
EPS = 1e-6
GELU_C = 0.7978845608028654
NSLOT = 4
SBJ = 2
NCORES = 8
PROBE_NOLOAD = False
PROBE_SKIP_FFN = False
PROBE_SKIP_MIXER = False


class Buf:
    def __init__(self, t, name):
        self.t = t
        self.name = name
        self.last_w = None
        self.reads = {}
        self.dma_sem = None
        self.dma_n = 0

    def __getitem__(self, k):
        return self.t[k]


class Eng:
    def __init__(self, name):
        self.name = name
        self.sem = None
        self.n = 0
        self.seen = {}
        self.prog = []


class Prog:
    def __init__(self, nc):
        self.nc = nc
        self.es = ExitStack()
        self.engs = {k: Eng(k) for k in ("pe", "act", "dve", "pool", "sp")}
        for k in ("pe", "act", "dve", "pool"):
            self.engs[k].sem = self.es.enter_context(nc.semaphore("s_" + k))

    def sbuf(self, shape, dtype, name, dma=False):
        t = self.es.enter_context(self.nc.sbuf_tensor("sb_" + name, list(shape), dtype))
        b = Buf(t, name)
        if dma:
            b.dma_sem = self.es.enter_context(self.nc.semaphore("d_" + name))
        return b

    def psum(self, shape, dtype, name):
        t = self.es.enter_context(self.nc.psum_tensor(name, list(shape), dtype))
        return Buf(t, name)

    def _wait(self, eng, tok, own_ok=False):
        if tok is None:
            return
        sem, val = tok
        if eng.sem is not None and sem is eng.sem and not own_ok:
            return
        key = id(sem)
        if eng.seen.get(key, 0) >= val:
            return
        eng.seen[key] = val
        eng.prog.append(("wait", sem, val))

    def op(self, ename, reads, writes, fn):
        eng = self.engs[ename]
        raw_own = ename in ("act", "dve", "pool")
        for b in reads:
            self._wait(eng, b.last_w, own_ok=raw_own)
        for b in writes:
            self._wait(eng, b.last_w)
            for sem, val in b.reads.values():
                self._wait(eng, (sem, val))
        eng.n += 1
        tok = (eng.sem, eng.n)
        eng.prog.append(("inst", fn, eng.sem, 1))
        for b in writes:
            b.last_w = tok
            b.reads = {}
        for b in reads:
            if b not in writes:
                b.reads[id(eng.sem)] = tok
        return tok

    def dma_load(self, qname, buf, fn):
        eng = self.engs[qname]
        lw = buf.last_w
        if lw is not None and lw[0] is not buf.dma_sem:
            self._wait(eng, lw)
        for sem, val in buf.reads.values():
            self._wait(eng, (sem, val))
        buf.dma_n += 1
        tok = (buf.dma_sem, 16 * buf.dma_n)
        eng.prog.append(("inst", fn, buf.dma_sem, 16))
        buf.last_w = tok
        buf.reads = {}

    def dma_store(self, qname, buf, fn):
        eng = self.engs[qname]
        self._wait(eng, buf.last_w)
        buf.dma_n += 1
        tok = (buf.dma_sem, 16 * buf.dma_n)
        eng.prog.append(("inst", fn, buf.dma_sem, 16))
        buf.reads[id(buf.dma_sem)] = tok

    def final_wait(self, qname, bufs):
        eng = self.engs[qname]
        for b in bufs:
            for sem, val in b.reads.values():
                self._wait(eng, (sem, val))
            self._wait(eng, b.last_w)

    def emit(self):
        engs = self.engs

        def replay(e, h):
            for it in e.prog:
                if it[0] == "wait":
                    h.wait_ge(it[1], it[2])
                else:
                    lst = it[1]
                    if isinstance(lst, tuple):
                        lst = [lst]
                    for (m, a, k) in lst:
                        ins = getattr(h, m)(*a, **k)
                    ins.then_inc(it[2], it[3])

        with self.nc.Block() as block:
            @block.tensor
            def _(h):
                replay(engs["pe"], h)

            @block.scalar
            def _(h):
                replay(engs["act"], h)

            @block.vector
            def _(h):
                replay(engs["dve"], h)

            @block.gpsimd
            def _(h):
                replay(engs["pool"], h)

            @block.sync
            def _(h):
                replay(engs["sp"], h)

    def close(self):
        self.es.close()


def I(m, *a, **k):
    return (m, a, k)


class Stream:
    def __init__(self):
        self.segs = [[]]

    def op(self, *a):
        self.segs[-1].append(("op", a))

    def dma_load(self, *a):
        self.segs[-1].append(("dma_load", a))

    def dma_store(self, *a):
        self.segs[-1].append(("dma_store", a))

    def cut(self):
        if self.segs[-1]:
            self.segs.append([])


def flush_segment(P, seg):
    for kind, a in seg:
        getattr(P, kind)(*a)


def merge_streams(P, A, B):
    sa = [x for x in A.segs if x]
    sb_ = [x for x in B.segs if x]
    ia = ib = 0
    while ia < len(sa) or ib < len(sb_):
        fa = (ia + 1) / len(sa) if ia < len(sa) else 2.0
        fb = (ib + 1) / len(sb_) if ib < len(sb_) else 2.0
        if fa <= fb:
            flush_segment(P, sa[ia])
            ia += 1
        else:
            flush_segment(P, sb_[ib])
            ib += 1


def tile_plan():
    tiles = [(128 * i, 128) for i in range(32)] + [(4096, 16)]
    groups = [list(range(3 * g, 3 * g + 3)) for g in range(11)]
    return tiles, groups


def build_nc(n_groups=None):
    nc = bass.Bass("TRN2", target_bir_lowering=False)
    dt = lambda name, shape, kind="ExternalInput": nc.dram_tensor(name, list(shape), F32, kind=kind).ap()
    x_d = dt("x", [SEQ, D])
    meta_d = dt("meta", [NMETA, D])
    win_d = dt("w_in", [D, DIN])
    wout_d = dt("w_out", [D, D])
    wgu_d = dt("w_gu", [D, 2 * DFF])
    wdn_d = dt("w_down", [DFF, D])
    gvec_d = dt("gvec", [3, D])
    gcol_d = dt("gcol", [128, 16])
    rgv_d = dt("rgv", [128, 36])
    wgate_d = dt("wgate", [128, 8 * 128])
    hglb_d = dt("hglb", [2, DHG])
    hgg_d = dt("hgg", [1, 128])
    out_d = dt("out", [SEQ, D], kind="ExternalOutput")

    tiles, groups = tile_plan()
    if n_groups is not None:
        groups = groups[:n_groups]

    P = Prog(nc)
    sb = P.sbuf
    w_in = sb([128, KC, DIN], BF16, "w_in", dma=True)
    w_out = sb([128, KC, D], BF16, "w_out", dma=True)
    ring = [dict(wg=sb([128, KC, 128], BF16, f"wg{i}", dma=True),
                 wu=sb([128, KC, 128], BF16, f"wu{i}", dma=True),
                 wd=sb([128, D], BF16, f"wd{i}", dma=True)) for i in range(NSLOT)]
    GM = 384
    hsets = [[sb([128, D], F32, f"hA{i}", dma=True) for i in range(3)], [sb([128, D], F32, f"hB{i}", dma=True) for i in range(3)]]
    aTs = [sb([128, KC, GM], BF16, f"aT{i}") for i in range(3)]
    yT = sb([128, KC, GM], BF16, "yT")
    xn32s = [sb([128, D], F32, f"xn32_{i}", dma=True) for i in range(3)]
    gfin = sb([128, D], F32, "gfin", dma=True)
    gcol = sb([128, 16], F32, "gcol", dma=True)
    Gt = [sb([128, GM], F32, f"G{i}") for i in range(2)]
    Ft = [sb([128, 512], F32, f"F{i}") for i in range(6)]
    Ht = [sb([128, 512], BF16, f"H{i}") for i in range(7)]
    Rt = [sb([128, GM], F32, f"R{i}") for i in range(6)]
    RH0 = sb([128, GM], BF16, "RH0")
    trT = sb([128, 16, 128], BF16, "trT")
    S = sb([128, DHG], F32, "S")
    Sb = sb([128, DHG], BF16, "Sb")
    xpad = sb([128, GM + 3], F32, "xpad")
    hist = sb([128, 4, 3], F32, "hist")
    hstate = sb([128, 4], F32, "hstate")
    actb = [sb([128, SBJ, GM], BF16, f"actb{i}") for i in range(2)]
    lbc = sb([128, 2, DHG], F32, "lbc", dma=True)
    ghg = sb([128, DHG], F32, "ghg", dma=True)
    mask4 = sb([128, 128], BF16, "mask4")
    Mqs = sb([128, 128], F32, "Mqs")
    Mkd = sb([128, 128], F32, "Mkd")
    Mka = sb([128, 128], F32, "Mka")
    Mkb = sb([128, 128], F32, "Mkb")
    identf = sb([128, 128], F32, "identf")
    ident = sb([128, 128], BF16, "ident")
    ones = sb([128, 1], F32, "ones")
    rgv = sb([128, 36], F32, "rgv", dma=True)
    rgc = sb([128, 16], F32, "rgc")
    wgate = sb([128, 8, 128], BF16, "wgate", dma=True)
    sm_ns = [sb([128, 4], F32, f"sm_n{i}") for i in range(3)]
    sm_h = sb([128, 8], F32, "sm_h")
    sm_rs = [sb([128, 8], F32, f"sm_r{i}") for i in range(3)]
    sm_f = sb([128, 4], F32, "sm_f")
    eB = sb([128, 4], F32, "eB")

    PS = P.es.enter_context(nc.psum_tensor("PS", [128, 8 * 512], F32))
    banks = [Buf(PS, f"bank{i}") for i in range(8)]
    bk = lambda i: PS[:, 512 * i:512 * (i + 1)]
    bkb = lambda i: PS[:, 512 * i:512 * (i + 1)].bitcast(BF16)
    rot = {"r": 0, "h": 0, "f": 0, "p": 0}
    FFN_BANKS = (4, 5)
    PRE_BANK = 6
    PP_BANKS = (0, 1, 2, 3)

    def nbr():
        rot["r"] = (rot["r"] + 1) % 2
        return rot["r"]

    def nbh():
        rot["h"] = (rot["h"] + 1) % 2
        return 2 + rot["h"]

    def nbp():
        rot["p"] = (rot["p"] + 1) % len(PP_BANKS)
        return PP_BANKS[rot["p"]]

    def nbf():
        rot["f"] = (rot["f"] + 1) % len(FFN_BANKS)
        return FFN_BANKS[rot["f"]]

    PIN = 7
    op = P.op

    for kc in range(KC):
        for c0 in (0, 1536):
            P.dma_load("pool", w_in, I("dma_start", out=w_in[:, kc, c0:c0 + 1536], in_=win_d[kc * 128:(kc + 1) * 128, c0:c0 + 1536]))
    for kc in range(KC):
        P.dma_load("pool", w_out, I("dma_start", out=w_out[:, kc, :], in_=wout_d[kc * 128:(kc + 1) * 128, :]))
    P.dma_load("pool", wgate, I("dma_start", out=wgate[:, :, :].rearrange("p a b -> p (a b)"), in_=wgate_d[:, :]))
    P.dma_load("sp", gfin, I("dma_start", out=gfin[:, :], in_=gvec_d[2:3, :].partition_broadcast(128)))
    P.dma_load("sp", gcol, I("dma_start", out=gcol[:, :], in_=gcol_d[:, :]))
    for i in range(2):
        P.dma_load("sp", lbc, I("dma_start", out=lbc[:, i, :], in_=hglb_d[i:i + 1, :].partition_broadcast(128)))
    for i in range(4):
        P.dma_load("sp", ghg, I("dma_start", out=ghg[:, i * 128:(i + 1) * 128], in_=hgg_d[0:1, :].partition_broadcast(128)))
    P.dma_load("sp", rgv, I("dma_start", out=rgv[:, :], in_=rgv_d[:, :]))

    op("pool", [], [identf], I("memset", identf[:, :], 1.0))
    op("pool", [identf], [identf], I("affine_select", out=identf[:, :], in_=identf[:, :], pattern=[[-1, 128]], compare_op=ALU.is_equal, fill=0.0, base=0, channel_multiplier=1))
    op("pool", [], [Mqs], I("memset", Mqs[:, :], 1.0))
    op("pool", [Mqs], [Mqs], I("affine_select", out=Mqs[:, :], in_=Mqs[:, :], pattern=[[1, 128]], compare_op=ALU.is_ge, fill=0.0, base=0, channel_multiplier=-1))
    op("pool", [], [Mkd], I("memset", Mkd[:, :], 1.0))
    op("pool", [Mkd], [Mkd], I("affine_select", out=Mkd[:, :], in_=Mkd[:, :], pattern=[[-1, 128]], compare_op=ALU.is_gt, fill=0.0, base=0, channel_multiplier=1))
    op("pool", [], [ones], I("memset", ones[:, :], 1.0))
    op("pool", [], [S], I("memset", S[:, :], 0.0))
    op("pool", [], [Sb], I("memset", Sb[:, :], 0.0))
    op("pool", [], [hist], I("memset", hist[:, :, :], 0.0))
    op("pool", [], [hstate], I("memset", hstate[:, :], 0.0))
    op("pool", [], [Mkb], I("memset", Mkb[:, :], 0.0))
    op("dve", [identf], [ident], I("tensor_copy", out=ident[:, :], in_=identf[:, :]))
    op("dve", [Mqs], [Mka], I("tensor_scalar", out=Mka[:, :], in0=Mqs[:, :], scalar1=-1.0, scalar2=None, op0=ALU.mult))
    op("dve", [Mka], [Mka], I("memset", Mka[0:64, 64:128], 0.0))
    op("dve", [Mkd, Mkb], [Mkb], I("tensor_copy", out=Mkb[0:64, 0:64], in_=Mkd[0:64, 0:64]))
    op("dve", [Mka, Mkb], [Mkb], I("tensor_copy", out=Mkb[64:128, 64:128], in_=Mka[64:128, 64:128]))
    op("dve", [Mqs], [mask4], I("tensor_copy", out=mask4[:, :], in_=Mqs[:, :]))
    op("dve", [rgv], [rgc], I("tensor_scalar", out=rgc[:, 0:8], in0=rgv[:, 20:28], scalar1=0.5, scalar2=None, op0=ALU.mult))
    op("act", [rgv], [rgc], I("activation", out=rgc[:, 8:12], in_=rgv[:, 28:32], func=AF.Exp, scale=-1.0))
    op("act", [rgc], [rgc], I("activation", out=rgc[:, 8:12], in_=rgc[:, 8:12], func=AF.Ln, bias=1.0))
    op("dve", [rgc], [rgc], I("tensor_scalar", out=rgc[:, 12:16], in0=rgc[:, 8:12], scalar1=-4.0, scalar2=None, op0=ALU.mult))
    op("dve", [rgc], [rgc], I("tensor_scalar", out=rgc[:, 8:12], in0=rgc[:, 8:12], scalar1=-8.0, scalar2=None, op0=ALU.mult))
    op("dve", [lbc], [lbc], I("tensor_tensor", out=lbc[:, 0, :], in0=lbc[:, 0, :], in1=lbc[:, 1, :], op=ALU.subtract))
    op("act", [lbc], [lbc], I("activation", out=lbc[:, 1, :], in_=lbc[:, 0, :], func=AF.Tanh, scale=0.5))
    op("dve", [lbc], [lbc], I("tensor_scalar", out=lbc[:, 0, :], in0=lbc[:, 1, :], scalar1=0.25, scalar2=0.75, op0=ALU.mult, op1=ALU.add))
    op("dve", [lbc], [lbc], I("tensor_scalar", out=lbc[:, 1, :], in0=lbc[:, 1, :], scalar1=-0.25, scalar2=0.25, op0=ALU.mult, op1=ALU.add))

    n_loads = len(groups) * NJ
    load_ptr = [0]
    wgu_src = wgu_d.rearrange("(kc p) n -> p kc n", p=128)

    def emit_ffn_load(st):
        L = load_ptr[0]
        if L >= n_loads or (PROBE_NOLOAD and L >= NSLOT):
            return
        load_ptr[0] += 1
        j = L % NJ
        slot = ring[L % NSLOT]
        wg, wu, wd = slot["wg"], slot["wu"], slot["wd"]
        st.dma_load("pool", wg, I("dma_start", out=wg[:, :, :], in_=wgu_src[:, :, j * 128:(j + 1) * 128]))
        st.dma_load("pool", wu, I("dma_start", out=wu[:, :, :], in_=wgu_src[:, :, DFF + j * 128:DFF + (j + 1) * 128]))
        st.dma_load("pool", wd, I("dma_start", out=wd[:, :], in_=wdn_d[j * 128:(j + 1) * 128, :]))

    def rstd_from_ss(st, smx, ss_ap, out_ap, width):
        st.op("act", [smx], [smx], I("activation", out=out_ap, in_=ss_ap, func=AF.Ln, scale=1.0 / width, bias=EPS))
        st.op("act", [smx], [smx], I("activation", out=out_ap, in_=out_ap, func=AF.Exp, scale=-0.5))

    def norm_to_aT(st, hb, nt, c0, gi, aT, li):
        xn32, sm_n, b = xn32s[li], sm_ns[li], li
        st.op("act", [hb], [xn32, sm_n], I("activation", out=xn32[:nt, :], in_=hb[:nt, :], func=AF.Square, accum_out=sm_n[:nt, 0:1]))
        rstd_from_ss(st, sm_n, sm_n[:nt, 0:1], sm_n[:nt, 1:2], D)
        st.op("dve", [hb, sm_n], [xn32], I("tensor_scalar", out=xn32[:nt, :], in0=hb[:nt, :], scalar1=sm_n[:nt, 1:2], scalar2=None, op0=ALU.mult))
        for i in range(2):
            st.op("pe", [xn32, identf], [banks[b]], [I("transpose", out=bk(b)[:, j * 128:j * 128 + nt], in_=xn32[:nt, (4 * i + j) * 128:(4 * i + j + 1) * 128], identity=identf[:nt, :nt]) for j in range(4)])
            st.op("dve", [banks[b], gcol], [aT], I("tensor_tensor", out=aT[:, 4 * i:4 * i + 4, c0:c0 + nt], in0=bk(b).rearrange("p (a b) -> p a b", a=4)[:, :, 0:nt],
                                                    in1=gcol[:, gi * 8 + 4 * i:gi * 8 + 4 * i + 4].unsqueeze(2).to_broadcast([128, 4, nt]), op=ALU.mult))

    def mm_acc(out_ap, pairs):
        n = len(pairs)
        return [I("matmul", out_ap, lhsT=l, rhs=r, start=(i == 0), stop=(i == n - 1)) for i, (l, r) in enumerate(pairs)]

    def group_info(g):
        gtiles = [tiles[i] for i in groups[g]]
        gp0 = gtiles[0][0]
        N = sum(nt for _, nt in gtiles)
        chunks = [(0, N)] if N <= 512 else [(0, N // 2), (N // 2, N - N // 2)]
        return gtiles, gp0, N, chunks

    def x_src(p0, nt):
        if p0 == 0:
            return [(slice(0, NMETA), meta_d[:, :]), (slice(NMETA, 128), x_d[0:128 - NMETA, :])]
        return [(slice(0, nt), x_d[p0 - NMETA:p0 - NMETA + nt, :])]

    def gen_xload(g, st):
        gtiles, gp0, N, chunks = group_info(g)
        for li, (p0, nt) in enumerate(gtiles):
            hb = hsets[g % 2][li]
            for sl, src in x_src(p0, nt):
                st.dma_load("sp", hb, I("dma_start", out=hb[sl, :], in_=src))

    def gen_pre_seq(g, st):
        gtiles, gp0, N, chunks = group_info(g)
        aT = aTs[g % 3]
        b = PRE_BANK
        for li, (p0, nt) in enumerate(gtiles):
            xn32, sm_n = xn32s[li], sm_ns[li]
            c0 = p0 - gp0
            for sl, src in x_src(p0, nt):
                st.dma_load("sp", xn32, I("dma_start", out=xn32[sl, :], in_=src))
            for half in range(2):
                st.op("act", [xn32], [banks[b], sm_n], I("activation", out=bk(b)[:nt, :], in_=xn32[:nt, half * 512:(half + 1) * 512], func=AF.Square, accum_out=sm_n[:nt, 2 + half:3 + half]))
            st.op("dve", [sm_n], [sm_n], I("tensor_tensor", out=sm_n[:nt, 0:1], in0=sm_n[:nt, 2:3], in1=sm_n[:nt, 3:4], op=ALU.add))
            rstd_from_ss(st, sm_n, sm_n[:nt, 0:1], sm_n[:nt, 1:2], D)
            st.op("dve", [xn32, sm_n], [xn32], I("tensor_scalar", out=xn32[:nt, :], in0=xn32[:nt, :], scalar1=sm_n[:nt, 1:2], scalar2=None, op0=ALU.mult))
            for i in range(2):
                st.op("pe", [xn32, identf], [banks[b]], [I("transpose", out=bk(b)[:, j * 128:j * 128 + nt], in_=xn32[:nt, (4 * i + j) * 128:(4 * i + j + 1) * 128], identity=identf[:nt, :nt]) for j in range(4)])
                st.op("dve", [banks[b], gcol], [aT], I("tensor_tensor", out=aT[:, 4 * i:4 * i + 4, c0:c0 + nt], in0=bk(b).rearrange("p (a b) -> p a b", a=4)[:, :, 0:nt],
                                                        in1=gcol[:, 4 * i:4 * i + 4].unsqueeze(2).to_broadcast([128, 4, nt]), op=ALU.mult))

    def gen_rg(g, st):
        gtiles, gp0, N, chunks = group_info(g)
        aT = aTs[g % 3]
        nbm = nbr
        F0, F1, F2, F3, F4, F5 = Rt
        H0 = RH0
        for blk in range(4):
            cw = [rgv[:, 4 * j + blk:4 * j + blk + 1] for j in range(4)]
            cb = rgv[:, 16 + blk:17 + blk]
            pX = [nbm() for _ in chunks]
            for ci, (cs, cn) in enumerate(chunks):
                col = blk * 128
                st.op("pe", [w_in, aT], [banks[pX[ci]]], mm_acc(bk(pX[ci])[:, 0:cn], [(w_in[:, kc, col:col + 128], aT[:, kc, cs:cs + cn]) for kc in range(KC)]))
            st.op("dve", [hist], [xpad], I("tensor_copy", out=xpad[:, 0:3], in_=hist[:, blk, :]))
            for ci, (cs, cn) in enumerate(chunks):
                st.op("act", [banks[pX[ci]]], [xpad], I("activation", out=xpad[:, 3 + cs:3 + cs + cn], in_=bk(pX[ci])[:, 0:cn], func=AF.Copy))
            pG = [nbm() for _ in chunks]
            for ci, (cs, cn) in enumerate(chunks):
                col = DRG + blk * 128
                st.op("pe", [w_in, aT], [banks[pG[ci]]], mm_acc(bk(pG[ci])[:, 0:cn], [(w_in[:, kc, col:col + 128], aT[:, kc, cs:cs + cn]) for kc in range(KC)]))
                st.op("act", [banks[pG[ci]]], [F5], I("activation", out=F5[:, cs:cs + cn], in_=bk(pG[ci])[:, 0:cn], func=AF.Copy))
            st.op("dve", [xpad, rgv], [F0], I("tensor_scalar", out=F0[:, 0:N], in0=xpad[:, 3:3 + N], scalar1=cw[3], scalar2=cb, op0=ALU.mult, op1=ALU.add))
            for j in (2, 1, 0):
                st.op("dve", [xpad, rgv, F0], [F0], I("scalar_tensor_tensor", out=F0[:, 0:N], in0=xpad[:, j:j + N], scalar=cw[j], in1=F0[:, 0:N], op0=ALU.mult, op1=ALU.add))
            st.op("dve", [xpad], [hist], I("tensor_copy", out=hist[:, blk, :], in_=xpad[:, N:N + 3]))
            st.op("act", [F0], [H0], I("activation", out=H0[:, 0:N], in_=F0[:, 0:N], func=AF.Copy))
            st.cut()
            for ci, (cs, cn) in enumerate(chunks):
                pR = nbm()
                st.op("pe", [wgate, H0], [banks[pR]], I("matmul", bk(pR)[:, 0:cn], lhsT=wgate[:, blk, :], rhs=H0[:, cs:cs + cn], start=True, stop=True))
                st.op("act", [banks[pR], rgc], [F1], I("activation", out=F1[:, cs:cs + cn], in_=bk(pR)[:, 0:cn], func=AF.Tanh, scale=0.5, bias=rgc[:, blk:blk + 1]))
                pI = nbm()
                st.op("pe", [wgate, H0], [banks[pI]], I("matmul", bk(pI)[:, 0:cn], lhsT=wgate[:, 4 + blk, :], rhs=H0[:, cs:cs + cn], start=True, stop=True))
                st.op("act", [banks[pI], rgc], [F2], I("activation", out=F2[:, cs:cs + cn], in_=bk(pI)[:, 0:cn], func=AF.Tanh, scale=0.5, bias=rgc[:, 4 + blk:5 + blk]))
            st.op("act", [F5], [F4], I("activation", out=F4[:, 0:N], in_=F5[:, 0:N], func=AF.Square))
            st.op("dve", [F4], [F4], I("tensor_scalar", out=F4[:, 0:N], in0=F4[:, 0:N], scalar1=0.044715, scalar2=1.0, op0=ALU.mult, op1=ALU.add))
            st.op("dve", [F4, F5], [F4], I("tensor_tensor", out=F4[:, 0:N], in0=F4[:, 0:N], in1=F5[:, 0:N], op=ALU.mult))
            st.op("act", [F4], [F4], I("activation", out=F4[:, 0:N], in_=F4[:, 0:N], func=AF.Tanh, scale=GELU_C))
            hcl = rgc[:, 12 + blk:13 + blk]
            cl = rgc[:, 8 + blk:9 + blk]
            st.op("act", [F1, rgc], [F3], I("activation", out=F3[:, 0:N], in_=F1[:, 0:N], func=AF.Exp, scale=hcl, bias=hcl))
            st.op("act", [F1, rgc], [F1], I("activation", out=F1[:, 0:N], in_=F1[:, 0:N], func=AF.Exp, scale=cl, bias=cl))
            st.op("act", [F1], [F1], I("activation", out=F1[:, 0:N], in_=F1[:, 0:N], func=AF.Ln, scale=-1.0, bias=1.0))
            st.op("act", [F1], [F1], I("activation", out=F1[:, 0:N], in_=F1[:, 0:N], func=AF.Exp, scale=0.5))
            st.op("dve", [F2, F0], [F2], I("scalar_tensor_tensor", out=F2[:, 0:N], in0=F2[:, 0:N], scalar=1.0, in1=F0[:, 0:N], op0=ALU.add, op1=ALU.mult))
            st.op("dve", [F2, F1], [F2], I("scalar_tensor_tensor", out=F2[:, 0:N], in0=F2[:, 0:N], scalar=0.5, in1=F1[:, 0:N], op0=ALU.mult, op1=ALU.mult))
            st.op("dve", [F3, F2, hstate], [F0], I("tensor_tensor_scan", out=F0[:, 0:N], data0=F3[:, 0:N], data1=F2[:, 0:N], initial=hstate[:, blk:blk + 1], op0=ALU.mult, op1=ALU.add))
            st.op("dve", [F0], [hstate], I("tensor_copy", out=hstate[:, blk:blk + 1], in_=F0[:, N - 1:N]))
            st.op("dve", [F4, F5], [F4], I("scalar_tensor_tensor", out=F4[:, 0:N], in0=F4[:, 0:N], scalar=1.0, in1=F5[:, 0:N], op0=ALU.add, op1=ALU.mult))
            st.op("dve", [F4, F0], [F4], I("scalar_tensor_tensor", out=F4[:, 0:N], in0=F4[:, 0:N], scalar=0.5, in1=F0[:, 0:N], op0=ALU.mult, op1=ALU.mult))
            st.op("dve", [F4, rgv], [yT], I("tensor_scalar", out=yT[:, blk, 0:N], in0=F4[:, 0:N], scalar1=rgv[:, 32 + blk:33 + blk], scalar2=None, op0=ALU.mult))
            st.op("act", [F4], [F2], I("activation", out=F2[:, 0:N], in_=F4[:, 0:N], func=AF.Square))
            st.cut()
            st.op("pe", [F2, ones], [banks[PIN]], [I("matmul", bk(PIN)[:nt, li * 4 + blk:li * 4 + blk + 1], lhsT=F2[:, p0 - gp0:p0 - gp0 + nt], rhs=ones[:, 0:1], start=True, stop=True) for li, (p0, nt) in enumerate(gtiles)])

    def gen_hg(g, st):
        gtiles, gp0, N, chunks = group_info(g)
        aT = aTs[g % 3]
        nbm = nbh
        for li, (p0, nt) in enumerate(gtiles):
            c0 = p0 - gp0
            fK, fLF, fQ, fG, fE0, fE1 = Ft
            hV, hKA, hQB, hKB, hQS, hKD, hAT = Ht
            st.cut()
            pf = nbm()
            st.op("pe", [aT, w_in], [banks[pf]], mm_acc(bk(pf)[:nt, :], [(aT[:, kc, c0:c0 + nt], w_in[:, kc, 1536:2048]) for kc in range(KC)]))
            st.op("act", [banks[pf]], [fK], I("activation", out=fK[:nt, 0:512], in_=bk(pf)[:nt, :], func=AF.Tanh, scale=0.5))
            pq = nbm()
            st.op("pe", [aT, w_in], [banks[pq]], mm_acc(bk(pq)[:nt, :], [(aT[:, kc, c0:c0 + nt], w_in[:, kc, 1024:1536]) for kc in range(KC)]))
            st.op("act", [banks[pq]], [fQ], I("activation", out=fQ[:nt, 0:512], in_=bk(pq)[:nt, :], func=AF.Silu))
            pg = nbm()
            st.op("pe", [aT, w_in], [banks[pg]], mm_acc(bk(pg)[:nt, :], [(aT[:, kc, c0:c0 + nt], w_in[:, kc, 2560:3072]) for kc in range(KC)]))
            st.op("act", [banks[pg]], [fG], I("activation", out=fG[:nt, 0:512], in_=bk(pg)[:nt, :], func=AF.Silu))
            pi_ = nbm()
            st.op("pe", [aT, w_in], [banks[pi_]], mm_acc(bk(pi_)[:nt, :], [(aT[:, kc, c0:c0 + nt], w_in[:, kc, 2048:2560]) for kc in range(KC)]))
            st.op("act", [banks[pi_]], [hV], I("activation", out=hV[:nt, 0:512], in_=bk(pi_)[:nt, :], func=AF.Copy))
            st.op("dve", [fK, lbc], [fK], I("tensor_tensor", out=fK[:nt, 0:512], in0=fK[:nt, 0:512], in1=lbc[:nt, 1, :], op=ALU.mult))
            st.op("dve", [fK, lbc], [fK], I("tensor_tensor", out=fK[:nt, 0:512], in0=fK[:nt, 0:512], in1=lbc[:nt, 0, :], op=ALU.add))
            st.op("act", [fK], [fLF], I("activation", out=fLF[:nt, 0:512], in_=fK[:nt, 0:512], func=AF.Ln))
            st.op("dve", [fK], [fK], I("tensor_scalar", out=fK[:nt, 0:512], in0=fK[:nt, 0:512], scalar1=-1.0, scalar2=1.0, op0=ALU.mult, op1=ALU.add))
            st.op("dve", [fG, ghg], [fG], I("tensor_tensor", out=fG[:nt, 0:512], in0=fG[:nt, 0:512], in1=ghg[:nt, :], op=ALU.mult))
            st.cut()
            st.op("pe", [fLF, ones], [banks[PIN]], [I("matmul", bk(PIN)[:, 32 + hh:33 + hh], lhsT=fLF[:nt, hh * 128:(hh + 1) * 128], rhs=ones[:nt, 0:1], start=True, stop=True) for hh in range(4)])
            st.op("act", [banks[PIN]], [eB], I("activation", out=eB[:, 0:4], in_=bk(PIN)[:, 32:36], func=AF.Exp))

            def expmul(M, scales_srcs_dsts):
                b = nbm()
                st.op("pe", [M, fLF], [banks[b]], I("matmul", bk(b)[:nt, :], lhsT=M[:nt, :nt], rhs=fLF[:nt, 0:512], start=True, stop=True))
                for scale, src, dst, tmp in scales_srcs_dsts:
                    st.op("act", [banks[b]], [tmp], I("activation", out=tmp[:nt, 0:512], in_=bk(b)[:nt, :], func=AF.Exp, scale=scale))
                    st.op("dve", [tmp, src], [dst], I("tensor_tensor", out=dst[:nt, 0:512], in0=src[:nt, 0:512], in1=tmp[:nt, 0:512], op=ALU.mult))
            expmul(Mka, [(1.0, fK, hKA, fE0), (-1.0, fQ, hQB, fE1)])
            expmul(Mkb, [(1.0, fK, hKB, fE0)])
            expmul(Mqs, [(1.0, fQ, hQS, fE1)])
            expmul(Mkd, [(1.0, fK, hKD, fE0)])
            st.cut()
            for half, srcs in ((0, (hQB, hQS)), (1, (hKA, hKB))):
                b = nbm()
                st.op("pe", [srcs[0], srcs[1], ident], [banks[b]], [I("transpose", out=bkb(b)[:, (vi * 4 + hh) * 128:(vi * 4 + hh) * 128 + nt], in_=s_[:nt, hh * 128:(hh + 1) * 128], identity=ident[:nt, :nt]) for vi, s_ in enumerate(srcs) for hh in range(4)])
                dst = trT[:, half * 8:half * 8 + 8, 0:nt]
                src = bkb(b).rearrange("p (a b) -> p a b", a=8)[:, :, 0:nt]
                if half == 0:
                    st.op("act", [banks[b]], [trT], I("activation", out=dst, in_=src, func=AF.Copy))
                else:
                    st.op("dve", [banks[b]], [trT], I("tensor_copy", out=dst, in_=src))
            st.cut()
            pA = nbm()
            n0 = min(64, nt)
            insA = []
            for hh in range(4):
                insA.append(I("matmul", bk(pA)[:nt, hh * 128:hh * 128 + n0], lhsT=trT[:, 8 + hh, 0:nt], rhs=trT[:, 0 + hh, 0:n0], start=True, stop=True))
                if nt > 64:
                    insA.append(I("matmul", bk(pA)[:nt, hh * 128 + 64:hh * 128 + nt], lhsT=trT[:, 12 + hh, 0:nt], rhs=trT[:, 0 + hh, 64:nt], start=True, stop=True))
            st.op("pe", [trT], [banks[pA]], insA)
            pU = nbm()
            st.op("pe", [hKD, hV], [banks[pU]], [I("matmul", bk(pU)[:, hh * 128:(hh + 1) * 128], lhsT=hKD[:nt, hh * 128:(hh + 1) * 128], rhs=hV[:nt, hh * 128:(hh + 1) * 128], start=True, stop=True) for hh in range(4)])
            if nt == 128:
                st.op("dve", [banks[pA], mask4], [hAT], I("tensor_tensor", out=hAT[:nt, 0:512].rearrange("p (a b) -> p a b", a=4), in0=bk(pA)[:nt, :].rearrange("p (a b) -> p a b", a=4), in1=mask4[:nt, :].unsqueeze(1).to_broadcast([nt, 4, 128]), op=ALU.mult))
            else:
                for hh in range(4):
                    st.op("dve", [banks[pA], mask4], [hAT], I("tensor_tensor", out=hAT[:nt, hh * 128:hh * 128 + nt], in0=bk(pA)[:nt, hh * 128:hh * 128 + nt], in1=mask4[:nt, 0:nt], op=ALU.mult))
            st.cut()
            pO = nbm()
            insO = []
            for hh in range(4):
                hs_ = slice(hh * 128, (hh + 1) * 128)
                insO.append(I("matmul", bk(pO)[:nt, hs_], lhsT=hAT[:nt, hh * 128:hh * 128 + nt], rhs=hV[:nt, hs_], start=True, stop=False))
                insO.append(I("matmul", bk(pO)[:nt, hs_], lhsT=trT[:, 4 + hh, 0:nt], rhs=Sb[:, hs_], start=False, stop=True))
            st.op("pe", [hAT, hV, trT, Sb], [banks[pO]], insO)
            for hh in range(4):
                hs_ = slice(hh * 128, (hh + 1) * 128)
                st.op("dve", [S, eB, banks[pU]], [S], I("scalar_tensor_tensor", out=S[:, hs_], in0=S[:, hs_], scalar=eB[:, hh:hh + 1], in1=bk(pU)[:, hs_], op0=ALU.mult, op1=ALU.add))
            st.op("act", [S], [Sb], I("activation", out=Sb[:, :], in_=S[:, :], func=AF.Copy))
            for hh in range(4):
                hs_ = slice(hh * 128, (hh + 1) * 128)
                st.op("act", [banks[pO]], [fE0, sm_h], I("activation", out=fE0[:nt, hs_], in_=bk(pO)[:nt, hs_], func=AF.Square, accum_out=sm_h[:nt, hh:hh + 1]))
            rstd_from_ss(st, sm_h, sm_h[:nt, 0:4], sm_h[:nt, 4:8], 128)
            st.op("dve", [banks[pO], sm_h], [fE1], I("tensor_tensor", out=fE1[:nt, 0:512].rearrange("p (a b) -> p a b", a=4), in0=bk(pO)[:nt, :].rearrange("p (a b) -> p a b", a=4), in1=sm_h[:nt, 4:8].unsqueeze(2).to_broadcast([nt, 4, 128]), op=ALU.mult))
            st.op("dve", [fE1, fG], [hKA], I("tensor_tensor", out=hKA[:nt, 0:512], in0=fE1[:nt, 0:512], in1=fG[:nt, 0:512], op=ALU.mult))
            st.cut()
            b = nbm()
            st.op("pe", [hKA, ident], [banks[b]], [I("transpose", out=bkb(b)[:, hh * 128:hh * 128 + nt], in_=hKA[:nt, hh * 128:(hh + 1) * 128], identity=ident[:nt, :nt]) for hh in range(4)])
            st.op("act", [banks[b]], [yT], I("activation", out=yT[:, 4:8, c0:c0 + nt], in_=bkb(b)[:, 0:512].rearrange("p (a b) -> p a b", a=4)[:, :, 0:nt], func=AF.Copy))

    def gen_post_tile(g, li, st):
        gtiles, gp0, N, chunks = group_info(g)
        hb = hsets[g % 2][li]
        aT = aTs[g % 3]
        sm_r = sm_rs[li]
        p0, nt = gtiles[li]
        c0 = p0 - gp0
        b = li
        st.op("dve", [banks[PIN]], [sm_r], I("tensor_copy", out=sm_r[:nt, 0:4], in_=bk(PIN)[:nt, li * 4:li * 4 + 4]))
        st.op("dve", [sm_r], [sm_r], I("tensor_tensor", out=sm_r[:nt, 4:6], in0=sm_r[:nt, 0:2], in1=sm_r[:nt, 2:4], op=ALU.add))
        st.op("dve", [sm_r], [sm_r], I("tensor_tensor", out=sm_r[:nt, 6:7], in0=sm_r[:nt, 4:5], in1=sm_r[:nt, 5:6], op=ALU.add))
        rstd_from_ss(st, sm_r, sm_r[:nt, 6:7], sm_r[:nt, 7:8], DRG)
        for half in range(2):
            cs_ = slice(half * 512, (half + 1) * 512)
            st.op("pe", [yT, w_out], [banks[b]], mm_acc(bk(b)[:nt, :], [(yT[:, blk, c0:c0 + nt], w_out[:, blk, cs_]) for blk in range(4)]))
            st.op("dve", [banks[b], sm_r, hb], [hb], I("scalar_tensor_tensor", out=hb[:nt, cs_], in0=bk(b)[:nt, :], scalar=sm_r[:nt, 7:8], in1=hb[:nt, cs_], op0=ALU.mult, op1=ALU.add))
            st.op("pe", [yT, w_out], [banks[b]], mm_acc(bk(b)[:nt, :], [(yT[:, blk, c0:c0 + nt], w_out[:, blk, cs_]) for blk in range(4, 8)]))
            st.op("dve", [banks[b], hb], [hb], I("tensor_tensor", out=hb[:nt, cs_], in0=bk(b)[:nt, :], in1=hb[:nt, cs_], op=ALU.add))
        norm_to_aT(st, hb, nt, c0, 1, aT, li)

    def gen_ffn(g, st):
        gtiles, gp0, N, chunks = group_info(g)
        hbuf = hsets[g % 2]
        aT = aTs[g % 3]
        for sbi in range(NJ // SBJ):
            ab = actb[sbi % 2]
            slots = []
            for jj in range(SBJ):
                L = g * NJ + sbi * SBJ + jj
                slot = ring[L % NSLOT]
                slots.append(slot)
                for ci, (cs, cn) in enumerate(chunks):
                    pg_, pu_ = nbf(), nbf()
                    st.op("pe", [slot["wg"], aT], [banks[pg_]], mm_acc(bk(pg_)[:, 0:cn], [(slot["wg"][:, kc, :], aT[:, kc, cs:cs + cn]) for kc in range(KC)]))
                    st.op("pe", [slot["wu"], aT], [banks[pu_]], mm_acc(bk(pu_)[:, 0:cn], [(slot["wu"][:, kc, :], aT[:, kc, cs:cs + cn]) for kc in range(KC)]))
                    tmp = Gt[(sbi * SBJ + jj) % 2]
                    st.op("act", [banks[pg_]], [tmp], I("activation", out=tmp[:, cs:cs + cn], in_=bk(pg_)[:, 0:cn], func=AF.Silu))
                    st.op("dve", [tmp, banks[pu_]], [ab], I("tensor_tensor", out=ab[:, jj, cs:cs + cn], in0=tmp[:, cs:cs + cn], in1=bk(pu_)[:, 0:cn], op=ALU.mult))
                    st.cut()
            for li, (p0, nt) in enumerate(gtiles):
                c0 = p0 - gp0
                hb = hbuf[li]
                for half in range(2):
                    pd = nbf()
                    cs_ = slice(half * 512, (half + 1) * 512)
                    st.op("pe", [ab] + [s_["wd"] for s_ in slots], [banks[pd]], mm_acc(bk(pd)[:nt, :], [(ab[:, jj, c0:c0 + nt], slots[jj]["wd"][:, cs_]) for jj in range(SBJ)]))
                    st.op("dve", [banks[pd], hb], [hb], I("tensor_tensor", out=hb[:nt, cs_], in0=bk(pd)[:nt, :], in1=hb[:nt, cs_], op=ALU.add))
                st.cut()
            for jj in range(SBJ):
                emit_ffn_load(st)
        for li, (p0, nt) in enumerate(gtiles):
            hb = hbuf[li]
            for half in range(2):
                jb = nbf()
                st.op("act", [hb], [banks[jb], sm_f], I("activation", out=bk(jb)[:nt, :], in_=hb[:nt, half * 512:(half + 1) * 512], func=AF.Square, accum_out=sm_f[:nt, 2 + half:3 + half]))
            st.op("dve", [sm_f], [sm_f], I("tensor_tensor", out=sm_f[:nt, 0:1], in0=sm_f[:nt, 2:3], in1=sm_f[:nt, 3:4], op=ALU.add))
            rstd_from_ss(st, sm_f, sm_f[:nt, 0:1], sm_f[:nt, 1:2], D)
            st.op("dve", [hb, sm_f, gfin], [hb], I("scalar_tensor_tensor", out=hb[:nt, :], in0=hb[:nt, :], scalar=sm_f[:nt, 1:2], in1=gfin[:nt, :], op0=ALU.mult, op1=ALU.mult))
            if p0 == 0:
                st.dma_store("sp", hb, I("dma_start", out=out_d[0:128 - NMETA, :], in_=hb[NMETA:128, :]))
            else:
                st.dma_store("sp", hb, I("dma_start", out=out_d[p0 - NMETA:p0 - NMETA + nt, :], in_=hb[0:nt, :]))
            st.cut()

    def flat(segs):
        return [it for seg in segs for it in seg]

    S_SET = (AF.Silu, AF.Tanh)
    L_SET = (AF.Exp, AF.Ln)

    def act_set(item):
        kind, a_ = item
        if kind != "op" or a_[0] != "act":
            return None
        insts = a_[3]
        if isinstance(insts, tuple):
            insts = [insts]
        f = insts[-1][2].get("func")
        if f in S_SET:
            return "S"
        if f in L_SET:
            return "L"
        return None

    ZIP_TOL = 0.03

    def zip_n(lists, tol=None):
        tol = ZIP_TOL if tol is None else tol
        lists = [l for l in lists if l]
        out = []
        idx = [0] * len(lists)
        cur = None
        while True:
            live = [k for k in range(len(lists)) if idx[k] < len(lists[k])]
            if not live:
                break
            frac = {k: (idx[k] + 1) / len(lists[k]) for k in live}
            fmin = min(frac.values())
            cands = [k for k in live if frac[k] <= fmin + tol]

            def cost(k):
                st_ = act_set(lists[k][idx[k]])
                return (1 if (st_ is not None and cur is not None and st_ != cur) else 0, frac[k])
            k = min(cands, key=cost)
            it = lists[k][idx[k]]
            st_ = act_set(it)
            if st_ is not None:
                cur = st_
            out.append(it)
            idx[k] += 1
        return out

    def zip_items(A, B):
        return zip_n([A, B])

    NG = len(groups)

    def mixer_items(g):
        if PROBE_SKIP_MIXER:
            return []
        ntile = len(groups[g])
        xl, rg, hg, pre = Stream(), Stream(), Stream(), Stream()
        gen_xload(g, xl)
        gen_rg(g, rg)
        gen_hg(g, hg)
        if g + 1 < NG:
            gen_pre_seq(g + 1, pre)
        posts = []
        for li in range(ntile):
            b_ = Stream()
            gen_post_tile(g, li, b_)
            posts.append(flat(b_.segs))
        return flat(xl.segs) + zip_n([flat(rg.segs), flat(hg.segs), flat(pre.segs)]) + zip_n(posts)

    st0 = Stream()
    for _ in range(NSLOT):
        emit_ffn_load(st0)
    gen_pre_seq(0, st0)
    flush_segment(P, flat(st0.segs) + mixer_items(0))
    for g in range(NG):
        sf = Stream()
        if not PROBE_SKIP_FFN:
            gen_ffn(g, sf)
        mit = mixer_items(g + 1) if g + 1 < NG else []
        flush_segment(P, zip_n([flat(sf.segs), mit]))
    P.final_wait("sp", hsets[0] + hsets[1])
    P.emit()
    P.close()
    return nc


def _layout_inputs(inp):
    f = lambda a: np.ascontiguousarray(np.asarray(a, dtype=np.float32))
    cw = f(inp["conv_w"])[0]
    cols = []
    for j in range(4):
        cols.append(cw[j].reshape(4, 128).T)
    for nm in ("conv_b", "b_rgate", "b_igate", "lru_lambda", "rg_norm_g"):
        cols.append(f(inp[nm])[0].reshape(4, 128).T)
    rgv = f(np.concatenate(cols, axis=1))
    wg = np.zeros((128, 8, 128), np.float32)
    for gi, nm in enumerate(("w_rgate", "w_igate")):
        w = f(inp[nm])[0]
        for blk in range(4):
            wg[0:64, gi * 4 + blk, 0:64] = w[2 * blk]
            wg[64:128, gi * 4 + blk, 64:128] = w[2 * blk + 1]
    shared = {
        "meta": f(inp["meta_tokens"]),
        "w_in": f(inp["w_in"])[0],
        "w_out": f(inp["w_out"])[0],
        "w_gu": f(inp["w_gate_up"])[0],
        "w_down": f(inp["w_down"])[0],
        "gvec": f(np.stack([f(inp["mix_norm_g"])[0], f(inp["ffn_norm_g"])[0], f(inp["final_norm_g"])], 0)),
        "rgv": rgv,
        "gcol": f(np.concatenate([f(inp["mix_norm_g"])[0].reshape(8, 128).T, f(inp["ffn_norm_g"])[0].reshape(8, 128).T], axis=1)),
        "wgate": f(wg.reshape(128, 8 * 128)),
        "hglb": f(inp["hg_lower_bound"]),
        "hgg": f(inp["hg_norm_g"]),
    }
    x = f(inp["x"])
    return [dict(shared, x=x[c]) for c in range(NCORES)]


def kernel(**inputs):
    in_maps = _layout_inputs(inputs)
    nc = build_nc()
    res = run_bass_kernel_spmd(nc, in_maps, core_ids=list(range(NCORES)))
    return np.stack([np.asarray(r["out"], dtype=np.float32) for r in res.results], axis=0)
```

```python
from contextlib import ExitStack
import numpy as np
import concourse.bass as bass
import concourse.mybir as mybir
from concourse.bass_utils import run_bass_kernel_spmd

F32 = mybir.dt.float32
BF16 = mybir.dt.bfloat16
AF = mybir.ActivationFunctionType
ALU = mybir.AluOpType

D = 1024
KC = 8
SEQ = 4096
NMETA = 16
T = SEQ + NMETA
DRG = 512
DHG = 512
DIN = 3072
DFF = 2816
NJ = DFF // 128
EPS = 1e-6
GELU_C = 0.7978845608028654
NSLOT = 4
SBJ = 4
NCORES = 8
PROBE_NOLOAD = False
PROBE_SKIP_FFN = False
PROBE_SKIP_MIXER = False


class Buf:
    def __init__(self, t, name):
        self.t = t
        self.name = name
        self.last_w = None
        self.reads = {}
        self.dma_sem = None
        self.dma_n = 0

    def __getitem__(self, k):
        return self.t[k]


class Eng:
    def __init__(self, name):
        self.name = name
        self.sem = None
        self.n = 0
        self.seen = {}
        self.prog = []


class Prog:
    def __init__(self, nc):
        self.nc = nc
        self.es = ExitStack()
        self.engs = {k: Eng(k) for k in ("pe", "act", "dve", "pool", "sp")}
        for k in ("pe", "act", "dve", "pool"):
            self.engs[k].sem = self.es.enter_context(nc.semaphore("s_" + k))

    def sbuf(self, shape, dtype, name, dma=False):
        t = self.es.enter_context(self.nc.sbuf_tensor("sb_" + name, list(shape), dtype))
        b = Buf(t, name)
        if dma:
            b.dma_sem = self.es.enter_context(self.nc.semaphore("d_" + name))
        return b

    def psum(self, shape, dtype, name):
        t = self.es.enter_context(self.nc.psum_tensor(name, list(shape), dtype))
        return Buf(t, name)

    def _wait(self, eng, tok, own_ok=False):
        if tok is None:
            return
        sem, val = tok
        if eng.sem is not None and sem is eng.sem and not own_ok:
            return
        key = id(sem)
        if eng.seen.get(key, 0) >= val:
            return
        eng.seen[key] = val
        eng.prog.append(("wait", sem, val))

    def op(self, ename, reads, writes, fn):
        eng = self.engs[ename]
        raw_own = ename in ("act", "dve", "pool")
        for b in reads:
            self._wait(eng, b.last_w, own_ok=raw_own)
        for b in writes:
            self._wait(eng, b.last_w)
            for sem, val in b.reads.values():
                self._wait(eng, (sem, val))
        eng.n += 1
        tok = (eng.sem, eng.n)
        eng.prog.append(("inst", fn, eng.sem, 1))
        for b in writes:
            b.last_w = tok
            b.reads = {}
        for b in reads:
            if b not in writes:
                b.reads[id(eng.sem)] = tok
        return tok

    def dma_load(self, qname, buf, fn):
        eng = self.engs[qname]
        lw = buf.last_w
        if lw is not None and lw[0] is not buf.dma_sem:
            self._wait(eng, lw)
        for sem, val in buf.reads.values():
            self._wait(eng, (sem, val))
        buf.dma_n += 1
        tok = (buf.dma_sem, 16 * buf.dma_n)
        eng.prog.append(("inst", fn, buf.dma_sem, 16))
        buf.last_w = tok
        buf.reads = {}

    def dma_store(self, qname, buf, fn):
        eng = self.engs[qname]
        self._wait(eng, buf.last_w)
        buf.dma_n += 1
        tok = (buf.dma_sem, 16 * buf.dma_n)
        eng.prog.append(("inst", fn, buf.dma_sem, 16))
        buf.reads[id(buf.dma_sem)] = tok

    def final_wait(self, qname, bufs):
        eng = self.engs[qname]
        for b in bufs:
            for sem, val in b.reads.values():
                self._wait(eng, (sem, val))
            self._wait(eng, b.last_w)

    def emit(self):
        engs = self.engs

        def replay(e, h):
            for it in e.prog:
                if it[0] == "wait":
                    h.wait_ge(it[1], it[2])
                else:
                    lst = it[1]
                    if isinstance(lst, tuple):
                        lst = [lst]
                    for (m, a, k) in lst:
                        ins = getattr(h, m)(*a, **k)
                    ins.then_inc(it[2], it[3])

        with self.nc.Block() as block:
            @block.tensor
            def _(h):
                replay(engs["pe"], h)

            @block.scalar
            def _(h):
                replay(engs["act"], h)

            @block.vector
            def _(h):
                replay(engs["dve"], h)

            @block.gpsimd
            def _(h):
                replay(engs["pool"], h)

            @block.sync
            def _(h):
                replay(engs["sp"], h)

    def close(self):
        self.es.close()


def I(m, *a, **k):
    return (m, a, k)


class Stream:
    def __init__(self):
        self.segs = [[]]

    def op(self, *a):
        self.segs[-1].append(("op", a))

    def dma_load(self, *a):
        self.segs[-1].append(("dma_load", a))

    def dma_store(self, *a):
        self.segs[-1].append(("dma_store", a))

    def cut(self):
        if self.segs[-1]:
            self.segs.append([])


def flush_segment(P, seg):
    for kind, a in seg:
        getattr(P, kind)(*a)


def merge_streams(P, A, B):
    sa = [x for x in A.segs if x]
    sb_ = [x for x in B.segs if x]
    ia = ib = 0
    while ia < len(sa) or ib < len(sb_):
        fa = (ia + 1) / len(sa) if ia < len(sa) else 2.0
        fb = (ib + 1) / len(sb_) if ib < len(sb_) else 2.0
        if fa <= fb:
            flush_segment(P, sa[ia])
            ia += 1
        else:
            flush_segment(P, sb_[ib])
            ib += 1


def tile_plan():
    tiles = [(128 * i, 128) for i in range(32)] + [(4096, 16)]
    groups = [list(range(3 * g, 3 * g + 3)) for g in range(11)]
    return tiles, groups


def build_nc(n_groups=None):
    nc = bass.Bass("TRN2", target_bir_lowering=False)
    dt = lambda name, shape, kind="ExternalInput": nc.dram_tensor(name, list(shape), F32, kind=kind).ap()
    x_d = dt("x", [SEQ, D])
    meta_d = dt("meta", [NMETA, D])
    win_d = dt("w_in", [D, DIN])
    wout_d = dt("w_out", [D, D])
    wgu_d = dt("w_gu", [D, 2 * DFF])
    wdn_d = dt("w_down", [DFF, D])
    gvec_d = dt("gvec", [3, D])
    gcol_d = dt("gcol", [128, 16])
    rgv_d = dt("rgv", [128, 36])
    wgate_d = dt("wgate", [128, 8 * 128])
    hglb_d = dt("hglb", [2, DHG])
    hgg_d = dt("hgg", [1, 128])
    out_d = dt("out", [SEQ, D], kind="ExternalOutput")

    tiles, groups = tile_plan()
    if n_groups is not None:
        groups = groups[:n_groups]

    P = Prog(nc)
    sb = P.sbuf
    w_in = sb([128, KC, DIN], BF16, "w_in", dma=True)
    w_out = sb([128, KC, D], BF16, "w_out", dma=True)
    NGU, NWD = 3, 7
    gu_ring = [dict(wg=sb([128, KC, 128], BF16, f"wg{i}", dma=True), wu=sb([128, KC, 128], BF16, f"wu{i}", dma=True)) for i in range(NGU)]
    wd_ring = [sb([128, D], BF16, f"wd{i}", dma=True) for i in range(NWD)]
    GM = 384
    hsets = [[sb([128, D], F32, f"hA{i}", dma=True) for i in range(3)], [sb([128, D], F32, f"hB{i}", dma=True) for i in range(3)]]
    aTs = [sb([128, KC, GM], BF16, "aT0"), sb([128, KC, GM], BF16, "aT1")]
    yT = sb([128, KC, GM], BF16, "yT")
    xn32s = [sb([128, D], F32, f"xn32_{i}") for i in range(3)]
    gfin = sb([128, D], F32, "gfin", dma=True)
    gcol = sb([128, 16], F32, "gcol", dma=True)
    Gt = [sb([128, GM], F32, f"G{i}") for i in range(2)]
    Ft = [sb([128, 512], F32, f"F{i}") for i in range(6)]
    Ht = [sb([128, 512], BF16, f"H{i}") for i in range(7)]
    Rt = [sb([128, GM], F32, f"R{i}") for i in range(6)]
    RH0 = sb([128, GM], BF16, "RH0")
    trT = sb([128, 16, 128], BF16, "trT")
    S = sb([128, DHG], F32, "S")
    Sb = sb([128, DHG], BF16, "Sb")
    xpad = sb([128, GM + 3], F32, "xpad")
    hist = sb([128, 4, 3], F32, "hist")
    hstate = sb([128, 4], F32, "hstate")
    actb = [sb([128, SBJ, GM], BF16, f"actb{i}") for i in range(2)]
    lbc = sb([128, 2, DHG], F32, "lbc", dma=True)
    ghg = sb([128, DHG], F32, "ghg", dma=True)
    mask4 = sb([128, 128], BF16, "mask4")
    Mqs = sb([128, 128], F32, "Mqs")
    Mkd = sb([128, 128], F32, "Mkd")
    Mka = sb([128, 128], F32, "Mka")
    Mkb = sb([128, 128], F32, "Mkb")
    identf = sb([128, 128], F32, "identf")
    ident = sb([128, 128], BF16, "ident")
    ones = sb([128, 1], F32, "ones")
    rgv = sb([128, 36], F32, "rgv", dma=True)
    rgc = sb([128, 16], F32, "rgc")
    wgate = sb([128, 8, 128], BF16, "wgate", dma=True)
    sm_ns = [sb([128, 2], F32, f"sm_n{i}") for i in range(3)]
    sm_h = sb([128, 8], F32, "sm_h")
    sm_rs = [sb([128, 8], F32, f"sm_r{i}") for i in range(3)]
    sm_f = sb([128, 4], F32, "sm_f")
    eB = sb([128, 4], F32, "eB")

    PS = P.es.enter_context(nc.psum_tensor("PS", [128, 8 * 512], F32))
    banks = [Buf(PS, f"bank{i}") for i in range(8)]
    bk = lambda i: PS[:, 512 * i:512 * (i + 1)]
    bkb = lambda i: PS[:, 512 * i:512 * (i + 1)].bitcast(BF16)
    rot = {"r": 0, "h": 0, "f": 0, "p": 0}
    FFN_BANKS = (4, 5, 6)
    PP_BANKS = (0, 1, 2, 3)

    def nbr():
        rot["r"] = (rot["r"] + 1) % 2
        return rot["r"]

    def nbh():
        rot["h"] = (rot["h"] + 1) % 2
        return 2 + rot["h"]

    def nbp():
        rot["p"] = (rot["p"] + 1) % len(PP_BANKS)
        return PP_BANKS[rot["p"]]

    def nbf():
        rot["f"] = (rot["f"] + 1) % len(FFN_BANKS)
        return FFN_BANKS[rot["f"]]

    PIN = 7
    op = P.op

    for kc in range(KC):
        for c0 in (0, 1536):
            P.dma_load("pool", w_in, I("dma_start", out=w_in[:, kc, c0:c0 + 1536], in_=win_d[kc * 128:(kc + 1) * 128, c0:c0 + 1536]))
    for kc in range(KC):
        P.dma_load("pool", w_out, I("dma_start", out=w_out[:, kc, :], in_=wout_d[kc * 128:(kc + 1) * 128, :]))
    P.dma_load("pool", wgate, I("dma_start", out=wgate[:, :, :].rearrange("p a b -> p (a b)"), in_=wgate_d[:, :]))
    P.dma_load("sp", gfin, I("dma_start", out=gfin[:, :], in_=gvec_d[2:3, :].partition_broadcast(128)))
    P.dma_load("sp", gcol, I("dma_start", out=gcol[:, :], in_=gcol_d[:, :]))
    for i in range(2):
        P.dma_load("sp", lbc, I("dma_start", out=lbc[:, i, :], in_=hglb_d[i:i + 1, :].partition_broadcast(128)))
    for i in range(4):
        P.dma_load("sp", ghg, I("dma_start", out=ghg[:, i * 128:(i + 1) * 128], in_=hgg_d[0:1, :].partition_broadcast(128)))
    P.dma_load("sp", rgv, I("dma_start", out=rgv[:, :], in_=rgv_d[:, :]))

    op("pool", [], [identf], I("memset", identf[:, :], 1.0))
    op("pool", [identf], [identf], I("affine_select", out=identf[:, :], in_=identf[:, :], pattern=[[-1, 128]], compare_op=ALU.is_equal, fill=0.0, base=0, channel_multiplier=1))
    op("pool", [], [Mqs], I("memset", Mqs[:, :], 1.0))
    op("pool", [Mqs], [Mqs], I("affine_select", out=Mqs[:, :], in_=Mqs[:, :], pattern=[[1, 128]], compare_op=ALU.is_ge, fill=0.0, base=0, channel_multiplier=-1))
    op("pool", [], [Mkd], I("memset", Mkd[:, :], 1.0))
    op("pool", [Mkd], [Mkd], I("affine_select", out=Mkd[:, :], in_=Mkd[:, :], pattern=[[-1, 128]], compare_op=ALU.is_gt, fill=0.0, base=0, channel_multiplier=1))
    op("pool", [], [ones], I("memset", ones[:, :], 1.0))
    op("pool", [], [S], I("memset", S[:, :], 0.0))
    op("pool", [], [Sb], I("memset", Sb[:, :], 0.0))
    op("pool", [], [hist], I("memset", hist[:, :, :], 0.0))
    op("pool", [], [hstate], I("memset", hstate[:, :], 0.0))
    op("pool", [], [Mkb], I("memset", Mkb[:, :], 0.0))
    op("dve", [identf], [ident], I("tensor_copy", out=ident[:, :], in_=identf[:, :]))
    op("dve", [Mqs], [Mka], I("tensor_scalar", out=Mka[:, :], in0=Mqs[:, :], scalar1=-1.0, scalar2=None, op0=ALU.mult))
    op("dve", [Mka], [Mka], I("memset", Mka[0:64, 64:128], 0.0))
    op("dve", [Mkd, Mkb], [Mkb], I("tensor_copy", out=Mkb[0:64, 0:64], in_=Mkd[0:64, 0:64]))
    op("dve", [Mka, Mkb], [Mkb], I("tensor_copy", out=Mkb[64:128, 64:128], in_=Mka[64:128, 64:128]))
    op("dve", [Mqs], [mask4], I("tensor_copy", out=mask4[:, :], in_=Mqs[:, :]))
    op("dve", [rgv], [rgc], I("tensor_scalar", out=rgc[:, 0:8], in0=rgv[:, 20:28], scalar1=0.5, scalar2=None, op0=ALU.mult))
    op("act", [rgv], [rgc], I("activation", out=rgc[:, 8:12], in_=rgv[:, 28:32], func=AF.Exp, scale=-1.0))
    op("act", [rgc], [rgc], I("activation", out=rgc[:, 8:12], in_=rgc[:, 8:12], func=AF.Ln, bias=1.0))
    op("dve", [rgc], [rgc], I("tensor_scalar", out=rgc[:, 12:16], in0=rgc[:, 8:12], scalar1=-4.0, scalar2=None, op0=ALU.mult))
    op("dve", [rgc], [rgc], I("tensor_scalar", out=rgc[:, 8:12], in0=rgc[:, 8:12], scalar1=-8.0, scalar2=None, op0=ALU.mult))
    op("dve", [lbc], [lbc], I("tensor_tensor", out=lbc[:, 0, :], in0=lbc[:, 0, :], in1=lbc[:, 1, :], op=ALU.subtract))
    op("act", [lbc], [lbc], I("activation", out=lbc[:, 1, :], in_=lbc[:, 0, :], func=AF.Tanh, scale=0.5))
    op("dve", [lbc], [lbc], I("tensor_scalar", out=lbc[:, 0, :], in0=lbc[:, 1, :], scalar1=0.25, scalar2=0.75, op0=ALU.mult, op1=ALU.add))
    op("dve", [lbc], [lbc], I("tensor_scalar", out=lbc[:, 1, :], in0=lbc[:, 1, :], scalar1=-0.25, scalar2=0.25, op0=ALU.mult, op1=ALU.add))

    n_loads = len(groups) * NJ
    wgu_src = wgu_d.rearrange("(kc p) n -> p kc n", p=128)

    def emit_gu_load(st, L):
        if L >= n_loads:
            return
        j = L % NJ
        slot = gu_ring[L % NGU]
        wg, wu = slot["wg"], slot["wu"]
        st.dma_load("pool", wg, I("dma_start", out=wg[:, :, :], in_=wgu_src[:, :, j * 128:(j + 1) * 128]))
        st.dma_load("pool", wu, I("dma_start", out=wu[:, :, :], in_=wgu_src[:, :, DFF + j * 128:DFF + (j + 1) * 128]))

    def emit_wd_load(st, L):
        if L >= n_loads:
            return
        j = L % NJ
        wd = wd_ring[L % NWD]
        st.dma_load("pool", wd, I("dma_start", out=wd[:, :], in_=wdn_d[j * 128:(j + 1) * 128, :]))

    def rstd_from_ss(st, smx, ss_ap, out_ap, width):
        st.op("act", [smx], [smx], I("activation", out=out_ap, in_=ss_ap, func=AF.Ln, scale=1.0 / width, bias=EPS))
        st.op("act", [smx], [smx], I("activation", out=out_ap, in_=out_ap, func=AF.Exp, scale=-0.5))

    def norm_to_aT(st, hb, nt, c0, gi, aT, li):
        xn32, sm_n, b = xn32s[li], sm_ns[li], li
        st.op("act", [hb], [xn32, sm_n], I("activation", out=xn32[:nt, :], in_=hb[:nt, :], func=AF.Square, accum_out=sm_n[:nt, 0:1]))
        rstd_from_ss(st, sm_n, sm_n[:nt, 0:1], sm_n[:nt, 1:2], D)
        st.op("dve", [hb, sm_n], [xn32], I("tensor_scalar", out=xn32[:nt, :], in0=hb[:nt, :], scalar1=sm_n[:nt, 1:2], scalar2=None, op0=ALU.mult))
        for i in range(2):
            st.op("pe", [xn32, identf], [banks[b]], [I("transpose", out=bk(b)[:, j * 128:j * 128 + nt], in_=xn32[:nt, (4 * i + j) * 128:(4 * i + j + 1) * 128], identity=identf[:nt, :nt]) for j in range(4)])
            st.op("dve", [banks[b], gcol], [aT], I("tensor_tensor", out=aT[:, 4 * i:4 * i + 4, c0:c0 + nt], in0=bk(b).rearrange("p (a b) -> p a b", a=4)[:, :, 0:nt],
                                                    in1=gcol[:, gi * 8 + 4 * i:gi * 8 + 4 * i + 4].unsqueeze(2).to_broadcast([128, 4, nt]), op=ALU.mult))

    def mm_acc(out_ap, pairs):
        n = len(pairs)
        return [I("matmul", out_ap, lhsT=l, rhs=r, start=(i == 0), stop=(i == n - 1)) for i, (l, r) in enumerate(pairs)]

    def group_info(g):
        gtiles = [tiles[i] for i in groups[g]]
        gp0 = gtiles[0][0]
        N = sum(nt for _, nt in gtiles)
        chunks = [(0, N)] if N <= 512 else [(0, N // 2), (N // 2, N - N // 2)]
        return gtiles, gp0, N, chunks

    def gen_pre_tile(g, li, st):
        gtiles, gp0, N, chunks = group_info(g)
        hb = hsets[g % 2][li]
        aT = aTs[g % 2]
        p0, nt = gtiles[li]
        if p0 == 0:
            st.dma_load("sp", hb, I("dma_start", out=hb[0:NMETA, :], in_=meta_d[:, :]))
            st.dma_load("sp", hb, I("dma_start", out=hb[NMETA:128, :], in_=x_d[0:128 - NMETA, :]))
        else:
            st.dma_load("sp", hb, I("dma_start", out=hb[0:nt, :], in_=x_d[p0 - NMETA:p0 - NMETA + nt, :]))
        norm_to_aT(st, hb, nt, p0 - gp0, 0, aT, li)

    def gen_rg(g, st):
        gtiles, gp0, N, chunks = group_info(g)
        aT = aTs[g % 2]
        nbm = nbr
        F0, F1, F2, F3, F4, F5 = Rt
        H0 = RH0
        for blk in range(4):
            cw = [rgv[:, 4 * j + blk:4 * j + blk + 1] for j in range(4)]
            cb = rgv[:, 16 + blk:17 + blk]
            pX = [nbm() for _ in chunks]
            for ci, (cs, cn) in enumerate(chunks):
                col = blk * 128
                st.op("pe", [w_in, aT], [banks[pX[ci]]], mm_acc(bk(pX[ci])[:, 0:cn], [(w_in[:, kc, col:col + 128], aT[:, kc, cs:cs + cn]) for kc in range(KC)]))
            st.op("dve", [hist], [xpad], I("tensor_copy", out=xpad[:, 0:3], in_=hist[:, blk, :]))
            for ci, (cs, cn) in enumerate(chunks):
                st.op("act", [banks[pX[ci]]], [xpad], I("activation", out=xpad[:, 3 + cs:3 + cs + cn], in_=bk(pX[ci])[:, 0:cn], func=AF.Copy))
            pG = [nbm() for _ in chunks]
            for ci, (cs, cn) in enumerate(chunks):
                col = DRG + blk * 128
                st.op("pe", [w_in, aT], [banks[pG[ci]]], mm_acc(bk(pG[ci])[:, 0:cn], [(w_in[:, kc, col:col + 128], aT[:, kc, cs:cs + cn]) for kc in range(KC)]))
                st.op("act", [banks[pG[ci]]], [F5], I("activation", out=F5[:, cs:cs + cn], in_=bk(pG[ci])[:, 0:cn], func=AF.Copy))
            st.op("dve", [xpad, rgv], [F0], I("tensor_scalar", out=F0[:, 0:N], in0=xpad[:, 3:3 + N], scalar1=cw[3], scalar2=cb, op0=ALU.mult, op1=ALU.add))
            for j in (2, 1, 0):
                st.op("dve", [xpad, rgv, F0], [F0], I("scalar_tensor_tensor", out=F0[:, 0:N], in0=xpad[:, j:j + N], scalar=cw[j], in1=F0[:, 0:N], op0=ALU.mult, op1=ALU.add))
            st.op("dve", [xpad], [hist], I("tensor_copy", out=hist[:, blk, :], in_=xpad[:, N:N + 3]))
            st.op("act", [F0], [H0], I("activation", out=H0[:, 0:N], in_=F0[:, 0:N], func=AF.Copy))
            st.cut()
            for ci, (cs, cn) in enumerate(chunks):
                pR = nbm()
                st.op("pe", [wgate, H0], [banks[pR]], I("matmul", bk(pR)[:, 0:cn], lhsT=wgate[:, blk, :], rhs=H0[:, cs:cs + cn], start=True, stop=True))
                st.op("act", [banks[pR], rgc], [F1], I("activation", out=F1[:, cs:cs + cn], in_=bk(pR)[:, 0:cn], func=AF.Tanh, scale=0.5, bias=rgc[:, blk:blk + 1]))
                pI = nbm()
                st.op("pe", [wgate, H0], [banks[pI]], I("matmul", bk(pI)[:, 0:cn], lhsT=wgate[:, 4 + blk, :], rhs=H0[:, cs:cs + cn], start=True, stop=True))
                st.op("act", [banks[pI], rgc], [F2], I("activation", out=F2[:, cs:cs + cn], in_=bk(pI)[:, 0:cn], func=AF.Tanh, scale=0.5, bias=rgc[:, 4 + blk:5 + blk]))
            st.op("act", [F5], [F4], I("activation", out=F4[:, 0:N], in_=F5[:, 0:N], func=AF.Square))
            st.op("dve", [F4], [F4], I("tensor_scalar", out=F4[:, 0:N], in0=F4[:, 0:N], scalar1=0.044715, scalar2=1.0, op0=ALU.mult, op1=ALU.add))
            st.op("dve", [F4, F5], [F4], I("tensor_tensor", out=F4[:, 0:N], in0=F4[:, 0:N], in1=F5[:, 0:N], op=ALU.mult))
            st.op("act", [F4], [F4], I("activation", out=F4[:, 0:N], in_=F4[:, 0:N], func=AF.Tanh, scale=GELU_C))
            hcl = rgc[:, 12 + blk:13 + blk]
            cl = rgc[:, 8 + blk:9 + blk]
            st.op("act", [F1, rgc], [F3], I("activation", out=F3[:, 0:N], in_=F1[:, 0:N], func=AF.Exp, scale=hcl, bias=hcl))
            st.op("act", [F1, rgc], [F1], I("activation", out=F1[:, 0:N], in_=F1[:, 0:N], func=AF.Exp, scale=cl, bias=cl))
            st.op("act", [F1], [F1], I("activation", out=F1[:, 0:N], in_=F1[:, 0:N], func=AF.Ln, scale=-1.0, bias=1.0))
            st.op("act", [F1], [F1], I("activation", out=F1[:, 0:N], in_=F1[:, 0:N], func=AF.Exp, scale=0.5))
            st.op("dve", [F2, F0], [F2], I("scalar_tensor_tensor", out=F2[:, 0:N], in0=F2[:, 0:N], scalar=1.0, in1=F0[:, 0:N], op0=ALU.add, op1=ALU.mult))
            st.op("dve", [F2, F1], [F2], I("scalar_tensor_tensor", out=F2[:, 0:N], in0=F2[:, 0:N], scalar=0.5, in1=F1[:, 0:N], op0=ALU.mult, op1=ALU.mult))
            st.op("dve", [F3, F2, hstate], [F0], I("tensor_tensor_scan", out=F0[:, 0:N], data0=F3[:, 0:N], data1=F2[:, 0:N], initial=hstate[:, blk:blk + 1], op0=ALU.mult, op1=ALU.add))
            st.op("dve", [F0], [hstate], I("tensor_copy", out=hstate[:, blk:blk + 1], in_=F0[:, N - 1:N]))
            st.op("dve", [F4, F5], [F4], I("scalar_tensor_tensor", out=F4[:, 0:N], in0=F4[:, 0:N], scalar=1.0, in1=F5[:, 0:N], op0=ALU.add, op1=ALU.mult))
            st.op("dve", [F4, F0], [F4], I("scalar_tensor_tensor", out=F4[:, 0:N], in0=F4[:, 0:N], scalar=0.5, in1=F0[:, 0:N], op0=ALU.mult, op1=ALU.mult))
            st.op("dve", [F4, rgv], [yT], I("tensor_scalar", out=yT[:, blk, 0:N], in0=F4[:, 0:N], scalar1=rgv[:, 32 + blk:33 + blk], scalar2=None, op0=ALU.mult))
            st.op("act", [F4], [F2], I("activation", out=F2[:, 0:N], in_=F4[:, 0:N], func=AF.Square))
            st.cut()
            st.op("pe", [F2, ones], [banks[PIN]], [I("matmul", bk(PIN)[:nt, li * 4 + blk:li * 4 + blk + 1], lhsT=F2[:, p0 - gp0:p0 - gp0 + nt], rhs=ones[:, 0:1], start=True, stop=True) for li, (p0, nt) in enumerate(gtiles)])

    def gen_hg(g, st):
        gtiles, gp0, N, chunks = group_info(g)
        aT = aTs[g % 2]
        nbm = nbh
        for li, (p0, nt) in enumerate(gtiles):
            c0 = p0 - gp0
            fK, fLF, fQ, fG, fE0, fE1 = Ft
            hV, hKA, hQB, hKB, hQS, hKD, hAT = Ht
            st.cut()
            pf = nbm()
            st.op("pe", [aT, w_in], [banks[pf]], mm_acc(bk(pf)[:nt, :], [(aT[:, kc, c0:c0 + nt], w_in[:, kc, 1536:2048]) for kc in range(KC)]))
            st.op("act", [banks[pf]], [fK], I("activation", out=fK[:nt, 0:512], in_=bk(pf)[:nt, :], func=AF.Tanh, scale=0.5))
            pq = nbm()
            st.op("pe", [aT, w_in], [banks[pq]], mm_acc(bk(pq)[:nt, :], [(aT[:, kc, c0:c0 + nt], w_in[:, kc, 1024:1536]) for kc in range(KC)]))
            st.op("act", [banks[pq]], [fQ], I("activation", out=fQ[:nt, 0:512], in_=bk(pq)[:nt, :], func=AF.Silu))
            pg = nbm()
            st.op("pe", [aT, w_in], [banks[pg]], mm_acc(bk(pg)[:nt, :], [(aT[:, kc, c0:c0 + nt], w_in[:, kc, 2560:3072]) for kc in range(KC)]))
            st.op("act", [banks[pg]], [fG], I("activation", out=fG[:nt, 0:512], in_=bk(pg)[:nt, :], func=AF.Silu))
            pi_ = nbm()
            st.op("pe", [aT, w_in], [banks[pi_]], mm_acc(bk(pi_)[:nt, :], [(aT[:, kc, c0:c0 + nt], w_in[:, kc, 2048:2560]) for kc in range(KC)]))
            st.op("act", [banks[pi_]], [hV], I("activation", out=hV[:nt, 0:512], in_=bk(pi_)[:nt, :], func=AF.Copy))
            st.op("dve", [fK, lbc], [fK], I("tensor_tensor", out=fK[:nt, 0:512], in0=fK[:nt, 0:512], in1=lbc[:nt, 1, :], op=ALU.mult))
            st.op("dve", [fK, lbc], [fK], I("tensor_tensor", out=fK[:nt, 0:512], in0=fK[:nt, 0:512], in1=lbc[:nt, 0, :], op=ALU.add))
            st.op("act", [fK], [fLF], I("activation", out=fLF[:nt, 0:512], in_=fK[:nt, 0:512], func=AF.Ln))
            st.op("dve", [fK], [fK], I("tensor_scalar", out=fK[:nt, 0:512], in0=fK[:nt, 0:512], scalar1=-1.0, scalar2=1.0, op0=ALU.mult, op1=ALU.add))
            st.op("dve", [fG, ghg], [fG], I("tensor_tensor", out=fG[:nt, 0:512], in0=fG[:nt, 0:512], in1=ghg[:nt, :], op=ALU.mult))
            st.cut()
            st.op("pe", [fLF, ones], [banks[PIN]], [I("matmul", bk(PIN)[:, 32 + hh:33 + hh], lhsT=fLF[:nt, hh * 128:(hh + 1) * 128], rhs=ones[:nt, 0:1], start=True, stop=True) for hh in range(4)])
            st.op("act", [banks[PIN]], [eB], I("activation", out=eB[:, 0:4], in_=bk(PIN)[:, 32:36], func=AF.Exp))

            def expmul(M, scales_srcs_dsts):
                b = nbm()
                st.op("pe", [M, fLF], [banks[b]], I("matmul", bk(b)[:nt, :], lhsT=M[:nt, :nt], rhs=fLF[:nt, 0:512], start=True, stop=True))
                for scale, src, dst, tmp in scales_srcs_dsts:
                    st.op("act", [banks[b]], [tmp], I("activation", out=tmp[:nt, 0:512], in_=bk(b)[:nt, :], func=AF.Exp, scale=scale))
                    st.op("dve", [tmp, src], [dst], I("tensor_tensor", out=dst[:nt, 0:512], in0=src[:nt, 0:512], in1=tmp[:nt, 0:512], op=ALU.mult))
            expmul(Mka, [(1.0, fK, hKA, fE0), (-1.0, fQ, hQB, fE1)])
            expmul(Mkb, [(1.0, fK, hKB, fE0)])
            expmul(Mqs, [(1.0, fQ, hQS, fE1)])
            expmul(Mkd, [(1.0, fK, hKD, fE0)])
            st.cut()
            for half, srcs in ((0, (hQB, hQS)), (1, (hKA, hKB))):
                b = nbm()
                st.op("pe", [srcs[0], srcs[1], ident], [banks[b]], [I("transpose", out=bkb(b)[:, (vi * 4 + hh) * 128:(vi * 4 + hh) * 128 + nt], in_=s_[:nt, hh * 128:(hh + 1) * 128], identity=ident[:nt, :nt]) for vi, s_ in enumerate(srcs) for hh in range(4)])
                dst = trT[:, half * 8:half * 8 + 8, 0:nt]
                src = bkb(b).rearrange("p (a b) -> p a b", a=8)[:, :, 0:nt]
                if half == 0:
                    st.op("act", [banks[b]], [trT], I("activation", out=dst, in_=src, func=AF.Copy))
                else:
                    st.op("dve", [banks[b]], [trT], I("tensor_copy", out=dst, in_=src))
            st.cut()
            pA = nbm()
            n0 = min(64, nt)
            insA = []
            for hh in range(4):
                insA.append(I("matmul", bk(pA)[:nt, hh * 128:hh * 128 + n0], lhsT=trT[:, 8 + hh, 0:nt], rhs=trT[:, 0 + hh, 0:n0], start=True, stop=True))
                if nt > 64:
                    insA.append(I("matmul", bk(pA)[:nt, hh * 128 + 64:hh * 128 + nt], lhsT=trT[:, 12 + hh, 0:nt], rhs=trT[:, 0 + hh, 64:nt], start=True, stop=True))
            st.op("pe", [trT], [banks[pA]], insA)
            pU = nbm()
            st.op("pe", [hKD, hV], [banks[pU]], [I("matmul", bk(pU)[:, hh * 128:(hh + 1) * 128], lhsT=hKD[:nt, hh * 128:(hh + 1) * 128], rhs=hV[:nt, hh * 128:(hh + 1) * 128], start=True, stop=True) for hh in range(4)])
            if nt == 128:
                st.op("dve", [banks[pA], mask4], [hAT], I("tensor_tensor", out=hAT[:nt, 0:512].rearrange("p (a b) -> p a b", a=4), in0=bk(pA)[:nt, :].rearrange("p (a b) -> p a b", a=4), in1=mask4[:nt, :].unsqueeze(1).to_broadcast([nt, 4, 128]), op=ALU.mult))
            else:
                for hh in range(4):
                    st.op("dve", [banks[pA], mask4], [hAT], I("tensor_tensor", out=hAT[:nt, hh * 128:hh * 128 + nt], in0=bk(pA)[:nt, hh * 128:hh * 128 + nt], in1=mask4[:nt, 0:nt], op=ALU.mult))
            st.cut()
            pO = nbm()
            insO = []
            for hh in range(4):
                hs_ = slice(hh * 128, (hh + 1) * 128)
                insO.append(I("matmul", bk(pO)[:nt, hs_], lhsT=hAT[:nt, hh * 128:hh * 128 + nt], rhs=hV[:nt, hs_], start=True, stop=False))
                insO.append(I("matmul", bk(pO)[:nt, hs_], lhsT=trT[:, 4 + hh, 0:nt], rhs=Sb[:, hs_], start=False, stop=True))
            st.op("pe", [hAT, hV, trT, Sb], [banks[pO]], insO)
            for hh in range(4):
                hs_ = slice(hh * 128, (hh + 1) * 128)
                st.op("dve", [S, eB, banks[pU]], [S], I("scalar_tensor_tensor", out=S[:, hs_], in0=S[:, hs_], scalar=eB[:, hh:hh + 1], in1=bk(pU)[:, hs_], op0=ALU.mult, op1=ALU.add))
            st.op("act", [S], [Sb], I("activation", out=Sb[:, :], in_=S[:, :], func=AF.Copy))
            for hh in range(4):
                hs_ = slice(hh * 128, (hh + 1) * 128)
                st.op("act", [banks[pO]], [fE0, sm_h], I("activation", out=fE0[:nt, hs_], in_=bk(pO)[:nt, hs_], func=AF.Square, accum_out=sm_h[:nt, hh:hh + 1]))
            rstd_from_ss(st, sm_h, sm_h[:nt, 0:4], sm_h[:nt, 4:8], 128)
            st.op("dve", [banks[pO], sm_h], [fE1], I("tensor_tensor", out=fE1[:nt, 0:512].rearrange("p (a b) -> p a b", a=4), in0=bk(pO)[:nt, :].rearrange("p (a b) -> p a b", a=4), in1=sm_h[:nt, 4:8].unsqueeze(2).to_broadcast([nt, 4, 128]), op=ALU.mult))
            st.op("dve", [fE1, fG], [hKA], I("tensor_tensor", out=hKA[:nt, 0:512], in0=fE1[:nt, 0:512], in1=fG[:nt, 0:512], op=ALU.mult))
            st.cut()
            b = nbm()
            st.op("pe", [hKA, ident], [banks[b]], [I("transpose", out=bkb(b)[:, hh * 128:hh * 128 + nt], in_=hKA[:nt, hh * 128:(hh + 1) * 128], identity=ident[:nt, :nt]) for hh in range(4)])
            st.op("act", [banks[b]], [yT], I("activation", out=yT[:, 4:8, c0:c0 + nt], in_=bkb(b)[:, 0:512].rearrange("p (a b) -> p a b", a=4)[:, :, 0:nt], func=AF.Copy))

    def gen_post_tile(g, li, st):
        gtiles, gp0, N, chunks = group_info(g)
        hb = hsets[g % 2][li]
        aT = aTs[g % 2]
        sm_r = sm_rs[li]
        p0, nt = gtiles[li]
        c0 = p0 - gp0
        b = li
        st.op("dve", [banks[PIN]], [sm_r], I("tensor_copy", out=sm_r[:nt, 0:4], in_=bk(PIN)[:nt, li * 4:li * 4 + 4]))
        st.op("dve", [sm_r], [sm_r], I("tensor_tensor", out=sm_r[:nt, 4:6], in0=sm_r[:nt, 0:2], in1=sm_r[:nt, 2:4], op=ALU.add))
        st.op("dve", [sm_r], [sm_r], I("tensor_tensor", out=sm_r[:nt, 6:7], in0=sm_r[:nt, 4:5], in1=sm_r[:nt, 5:6], op=ALU.add))
        rstd_from_ss(st, sm_r, sm_r[:nt, 6:7], sm_r[:nt, 7:8], DRG)
        for half in range(2):
            cs_ = slice(half * 512, (half + 1) * 512)
            st.op("pe", [yT, w_out], [banks[b]], mm_acc(bk(b)[:nt, :], [(yT[:, blk, c0:c0 + nt], w_out[:, blk, cs_]) for blk in range(4)]))
            st.op("dve", [banks[b], sm_r, hb], [hb], I("scalar_tensor_tensor", out=hb[:nt, cs_], in0=bk(b)[:nt, :], scalar=sm_r[:nt, 7:8], in1=hb[:nt, cs_], op0=ALU.mult, op1=ALU.add))
            st.op("pe", [yT, w_out], [banks[b]], mm_acc(bk(b)[:nt, :], [(yT[:, blk, c0:c0 + nt], w_out[:, blk, cs_]) for blk in range(4, 8)]))
            st.op("dve", [banks[b], hb], [hb], I("tensor_tensor", out=hb[:nt, cs_], in0=bk(b)[:nt, :], in1=hb[:nt, cs_], op=ALU.add))
        norm_to_aT(st, hb, nt, c0, 1, aT, li)

    def gen_ffn(g, st):
        gtiles, gp0, N, chunks = group_info(g)
        hbuf = hsets[g % 2]
        aT = aTs[g % 2]
        j0 = 0
        sbi = 0
        while j0 < NJ:
            nj = min(SBJ, NJ - j0)
            ab = actb[sbi % 2]
            for jj in range(nj):
                L = g * NJ + j0 + jj
                slot = gu_ring[L % NGU]
                for ci, (cs, cn) in enumerate(chunks):
                    pg_, pu_ = nbf(), nbf()
                    st.op("pe", [slot["wg"], aT], [banks[pg_]], mm_acc(bk(pg_)[:, 0:cn], [(slot["wg"][:, kc, :], aT[:, kc, cs:cs + cn]) for kc in range(KC)]))
                    st.op("pe", [slot["wu"], aT], [banks[pu_]], mm_acc(bk(pu_)[:, 0:cn], [(slot["wu"][:, kc, :], aT[:, kc, cs:cs + cn]) for kc in range(KC)]))
                    tmp = Gt[jj % 2]
                    st.op("act", [banks[pg_]], [tmp], I("activation", out=tmp[:, cs:cs + cn], in_=bk(pg_)[:, 0:cn], func=AF.Silu))
                    st.op("dve", [tmp, banks[pu_]], [ab], I("tensor_tensor", out=ab[:, jj, cs:cs + cn], in0=tmp[:, cs:cs + cn], in1=bk(pu_)[:, 0:cn], op=ALU.mult))
                emit_gu_load(st, L + NGU)
            wds = [wd_ring[(g * NJ + j0 + jj) % NWD] for jj in range(nj)]
            for li, (p0, nt) in enumerate(gtiles):
                c0 = p0 - gp0
                hb = hbuf[li]
                for half in range(2):
                    pd = nbf()
                    cs_ = slice(half * 512, (half + 1) * 512)
                    st.op("pe", [ab] + wds, [banks[pd]], mm_acc(bk(pd)[:nt, :], [(ab[:, jj, c0:c0 + nt], wds[jj][:, cs_]) for jj in range(nj)]))
                    st.op("dve", [banks[pd], hb], [hb], I("tensor_tensor", out=hb[:nt, cs_], in0=bk(pd)[:nt, :], in1=hb[:nt, cs_], op=ALU.add))
            for jj in range(nj):
                emit_wd_load(st, g * NJ + j0 + jj + NWD)
            j0 += nj
            sbi += 1
        for li, (p0, nt) in enumerate(gtiles):
            hb = hbuf[li]
            for half in range(2):
                jb = nbf()
                st.op("act", [hb], [banks[jb], sm_f], I("activation", out=bk(jb)[:nt, :], in_=hb[:nt, half * 512:(half + 1) * 512], func=AF.Square, accum_out=sm_f[:nt, 2 + half:3 + half]))
            st.op("dve", [sm_f], [sm_f], I("tensor_tensor", out=sm_f[:nt, 0:1], in0=sm_f[:nt, 2:3], in1=sm_f[:nt, 3:4], op=ALU.add))
            rstd_from_ss(st, sm_f, sm_f[:nt, 0:1], sm_f[:nt, 1:2], D)
            st.op("dve", [hb, sm_f, gfin], [hb], I("scalar_tensor_tensor", out=hb[:nt, :], in0=hb[:nt, :], scalar=sm_f[:nt, 1:2], in1=gfin[:nt, :], op0=ALU.mult, op1=ALU.mult))
            if p0 == 0:
                st.dma_store("sp", hb, I("dma_start", out=out_d[0:128 - NMETA, :], in_=hb[NMETA:128, :]))
            else:
                st.dma_store("sp", hb, I("dma_start", out=out_d[p0 - NMETA:p0 - NMETA + nt, :], in_=hb[0:nt, :]))
            st.cut()

    def flat(segs):
        return [it for seg in segs for it in seg]

    S_SET = (AF.Silu, AF.Tanh)
    L_SET = (AF.Exp, AF.Ln)

    def act_set(item):
        kind, a_ = item
        if kind != "op" or a_[0] != "act":
            return None
        insts = a_[3]
        if isinstance(insts, tuple):
            insts = [insts]
        f = insts[-1][2].get("func")
        if f in S_SET:
            return "S"
        if f in L_SET:
            return "L"
        return None

    ZIP_TOL = 0.03

    def zip_n(lists, tol=None):
        tol = ZIP_TOL if tol is None else tol
        lists = [l for l in lists if l]
        out = []
        idx = [0] * len(lists)
        cur = None
        while True:
            live = [k for k in range(len(lists)) if idx[k] < len(lists[k])]
            if not live:
                break
            frac = {k: (idx[k] + 1) / len(lists[k]) for k in live}
            fmin = min(frac.values())
            cands = [k for k in live if frac[k] <= fmin + tol]

            def cost(k):
                st_ = act_set(lists[k][idx[k]])
                return (1 if (st_ is not None and cur is not None and st_ != cur) else 0, frac[k])
            k = min(cands, key=cost)
            it = lists[k][idx[k]]
            st_ = act_set(it)
            if st_ is not None:
                cur = st_
            out.append(it)
            idx[k] += 1
        return out

    def zip_items(A, B):
        return zip_n([A, B])

    def mixer_items(g):
        if PROBE_SKIP_MIXER:
            return []
        ntile = len(groups[g])
        pres, posts = [], []
        for li in range(ntile):
            a_, b_ = Stream(), Stream()
            gen_pre_tile(g, li, a_)
            gen_post_tile(g, li, b_)
            pres.append(flat(a_.segs))
            posts.append(flat(b_.segs))
        rg, hg = Stream(), Stream()
        gen_rg(g, rg)
        gen_hg(g, hg)
        return zip_n(pres) + zip_items(flat(rg.segs), flat(hg.segs)) + zip_n(posts)

    NG = len(groups)
    st0 = Stream()
    for L0 in range(NGU):
        emit_gu_load(st0, L0)
    for L0 in range(NWD):
        emit_wd_load(st0, L0)
    flush_segment(P, flat(st0.segs) + mixer_items(0))
    for g in range(NG):
        sf = Stream()
        if not PROBE_SKIP_FFN:
            gen_ffn(g, sf)
        mit = mixer_items(g + 1) if g + 1 < NG else []
        flush_segment(P, zip_items(flat(sf.segs), mit))
    P.final_wait("sp", hsets[0] + hsets[1])
    P.emit()
    P.close()
    return nc


def _layout_inputs(inp):
    f = lambda a: np.ascontiguousarray(np.asarray(a, dtype=np.float32))
    cw = f(inp["conv_w"])[0]
    cols = []
    for j in range(4):
        cols.append(cw[j].reshape(4, 128).T)
    for nm in ("conv_b", "b_rgate", "b_igate", "lru_lambda", "rg_norm_g"):
        cols.append(f(inp[nm])[0].reshape(4, 128).T)
    rgv = f(np.concatenate(cols, axis=1))
    wg = np.zeros((128, 8, 128), np.float32)
    for gi, nm in enumerate(("w_rgate", "w_igate")):
        w = f(inp[nm])[0]
        for blk in range(4):
            wg[0:64, gi * 4 + blk, 0:64] = w[2 * blk]
            wg[64:128, gi * 4 + blk, 64:128] = w[2 * blk + 1]
    shared = {
        "meta": f(inp["meta_tokens"]),
        "w_in": f(inp["w_in"])[0],
        "w_out": f(inp["w_out"])[0],
        "w_gu": f(inp["w_gate_up"])[0],
        "w_down": f(inp["w_down"])[0],
        "gvec": f(np.stack([f(inp["mix_norm_g"])[0], f(inp["ffn_norm_g"])[0], f(inp["final_norm_g"])], 0)),
        "rgv": rgv,
        "gcol": f(np.concatenate([f(inp["mix_norm_g"])[0].reshape(8, 128).T, f(inp["ffn_norm_g"])[0].reshape(8, 128).T], axis=1)),
        "wgate": f(wg.reshape(128, 8 * 128)),
        "hglb": f(inp["hg_lower_bound"]),
        "hgg": f(inp["hg_norm_g"]),
    }
    x = f(inp["x"])
    return [dict(shared, x=x[c]) for c in range(NCORES)]


def kernel(**inputs):
    in_maps = _layout_inputs(inputs)
    nc = build_nc()
    res = run_bass_kernel_spmd(nc, in_maps, core_ids=list(range(NCORES)))
    return np.stack([np.asarray(r["out"], dtype=np.float32) for r in res.results], axis=0)
```

```python
from contextlib import ExitStack
import numpy as np
import concourse.bass as bass
import concourse.mybir as mybir
from concourse.bass_utils import run_bass_kernel_spmd

F32 = mybir.dt.float32
BF16 = mybir.dt.bfloat16
AF = mybir.ActivationFunctionType
ALU = mybir.AluOpType

D = 1024
KC = 8
SEQ = 4096
NMETA = 16
T = SEQ + NMETA
DRG = 512
DHG = 512
DIN = 3072
DFF = 2816
NJ = DFF // 128
EPS = 1e-6
GELU_C = 0.7978845608028654
NSLOT = 4
SBJ = 5
NCORES = 8
PROBE_NOLOAD = False
PROBE_SKIP_FFN = False
PROBE_SKIP_MIXER = False


class Buf:
    def __init__(self, t, name):
        self.t = t
        self.name = name
        self.last_w = None
        self.reads = {}
        self.dma_sem = None
        self.dma_n = 0

    def __getitem__(self, k):
        return self.t[k]


class Eng:
    def __init__(self, name):
        self.name = name
        self.sem = None
        self.n = 0
        self.seen = {}
        self.prog = []


class Prog:
    def __init__(self, nc):
        self.nc = nc
        self.es = ExitStack()
        self.engs = {k: Eng(k) for k in ("pe", "act", "dve", "pool", "sp")}
        for k in ("pe", "act", "dve", "pool"):
            self.engs[k].sem = self.es.enter_context(nc.semaphore("s_" + k))

    def sbuf(self, shape, dtype, name, dma=False):
        t = self.es.enter_context(self.nc.sbuf_tensor("sb_" + name, list(shape), dtype))
        b = Buf(t, name)
        if dma:
            b.dma_sem = self.es.enter_context(self.nc.semaphore("d_" + name))
        return b

    def psum(self, shape, dtype, name):
        t = self.es.enter_context(self.nc.psum_tensor(name, list(shape), dtype))
        return Buf(t, name)

    def _wait(self, eng, tok, own_ok=False):
        if tok is None:
            return
        sem, val = tok
        if eng.sem is not None and sem is eng.sem and not own_ok:
            return
        key = id(sem)
        if eng.seen.get(key, 0) >= val:
            return
        eng.seen[key] = val
        eng.prog.append(("wait", sem, val))

    def op(self, ename, reads, writes, fn):
        eng = self.engs[ename]
        raw_own = ename in ("act", "dve", "pool")
        for b in reads:
            self._wait(eng, b.last_w, own_ok=raw_own)
        for b in writes:
            self._wait(eng, b.last_w)
            for sem, val in b.reads.values():
                self._wait(eng, (sem, val))
        eng.n += 1
        tok = (eng.sem, eng.n)
        eng.prog.append(("inst", fn, eng.sem, 1))
        for b in writes:
            b.last_w = tok
            b.reads = {}
        for b in reads:
            if b not in writes:
                b.reads[id(eng.sem)] = tok
        return tok

    def dma_load(self, qname, buf, fn):
        eng = self.engs[qname]
        lw = buf.last_w
        if lw is not None and lw[0] is not buf.dma_sem:
            self._wait(eng, lw)
        for sem, val in buf.reads.values():
            self._wait(eng, (sem, val))
        buf.dma_n += 1
        tok = (buf.dma_sem, 16 * buf.dma_n)
        eng.prog.append(("inst", fn, buf.dma_sem, 16))
        buf.last_w = tok
        buf.reads = {}

    def dma_store(self, qname, buf, fn):
        eng = self.engs[qname]
        self._wait(eng, buf.last_w)
        buf.dma_n += 1
        tok = (buf.dma_sem, 16 * buf.dma_n)
        eng.prog.append(("inst", fn, buf.dma_sem, 16))
        buf.reads[id(buf.dma_sem)] = tok

    def final_wait(self, qname, bufs):
        eng = self.engs[qname]
        for b in bufs:
            for sem, val in b.reads.values():
                self._wait(eng, (sem, val))
            self._wait(eng, b.last_w)

    def emit(self):
        engs = self.engs

        def replay(e, h):
            for it in e.prog:
                if it[0] == "wait":
                    h.wait_ge(it[1], it[2])
                else:
                    lst = it[1]
                    if isinstance(lst, tuple):
                        lst = [lst]
                    for (m, a, k) in lst:
                        ins = getattr(h, m)(*a, **k)
                    ins.then_inc(it[2], it[3])

        with self.nc.Block() as block:
            @block.tensor
            def _(h):
                replay(engs["pe"], h)

            @block.scalar
            def _(h):
                replay(engs["act"], h)

            @block.vector
            def _(h):
                replay(engs["dve"], h)

            @block.gpsimd
            def _(h):
                replay(engs["pool"], h)

            @block.sync
            def _(h):
                replay(engs["sp"], h)

    def close(self):
        self.es.close()


def I(m, *a, **k):
    return (m, a, k)


class Stream:
    def __init__(self):
        self.segs = [[]]

    def op(self, *a):
        self.segs[-1].append(("op", a))

    def dma_load(self, *a):
        self.segs[-1].append(("dma_load", a))

    def dma_store(self, *a):
        self.segs[-1].append(("dma_store", a))

    def cut(self):
        if self.segs[-1]:
            self.segs.append([])


def flush_segment(P, seg):
    for kind, a in seg:
        getattr(P, kind)(*a)


def merge_streams(P, A, B):
    sa = [x for x in A.segs if x]
    sb_ = [x for x in B.segs if x]
    ia = ib = 0
    while ia < len(sa) or ib < len(sb_):
        fa = (ia + 1) / len(sa) if ia < len(sa) else 2.0
        fb = (ib + 1) / len(sb_) if ib < len(sb_) else 2.0
        if fa <= fb:
            flush_segment(P, sa[ia])
            ia += 1
        else:
            flush_segment(P, sb_[ib])
            ib += 1


def tile_plan():
    tiles = [(128 * i, 128) for i in range(32)] + [(4096, 16)]
    groups = [list(range(3 * g, 3 * g + 3)) for g in range(11)]
    return tiles, groups


def build_nc(n_groups=None):
    nc = bass.Bass("TRN2", target_bir_lowering=False)
    dt = lambda name, shape, kind="ExternalInput": nc.dram_tensor(name, list(shape), F32, kind=kind).ap()
    x_d = dt("x", [SEQ, D])
    meta_d = dt("meta", [NMETA, D])
    win_d = dt("w_in", [D, DIN])
    wout_d = dt("w_out", [D, D])
    wgu_d = dt("w_gu", [D, 2 * DFF])
    wdn_d = dt("w_down", [DFF, D])
    gvec_d = dt("gvec", [3, D])
    gcol_d = dt("gcol", [128, 16])
    rgv_d = dt("rgv", [128, 36])
    wgate_d = dt("wgate", [128, 8 * 128])
    hglb_d = dt("hglb", [2, DHG])
    hgg_d = dt("hgg", [1, 128])
    out_d = dt("out", [SEQ, D], kind="ExternalOutput")

    tiles, groups = tile_plan()
    if n_groups is not None:
        groups = groups[:n_groups]

    P = Prog(nc)
    sb = P.sbuf
    w_in = sb([128, KC, DIN], BF16, "w_in", dma=True)
    w_out = sb([128, KC, D], BF16, "w_out", dma=True)
    NGU, NWD = 3, 7
    gu_ring = [dict(wg=sb([128, KC, 128], BF16, f"wg{i}", dma=True), wu=sb([128, KC, 128], BF16, f"wu{i}", dma=True)) for i in range(NGU)]
    wd_ring = [sb([128, D], BF16, f"wd{i}", dma=True) for i in range(NWD)]
    GM = 384
    hsets = [[sb([128, D], F32, f"hA{i}", dma=True) for i in range(3)], [sb([128, D], F32, f"hB{i}", dma=True) for i in range(3)]]
    aTs = [sb([128, KC, GM], BF16, "aT0"), sb([128, KC, GM], BF16, "aT1")]
    yT = sb([128, KC, GM], BF16, "yT")
    xn32s = [sb([128, D], F32, f"xn32_{i}") for i in range(3)]
    gfin = sb([128, D], F32, "gfin", dma=True)
    gcol = sb([128, 16], F32, "gcol", dma=True)
    Gt = [sb([128, GM], F32, f"G{i}") for i in range(2)]
    Ft = [sb([128, 512], F32, f"F{i}") for i in range(6)]
    Ht = [sb([128, 512], BF16, f"H{i}") for i in range(7)]
    Rt = [sb([128, GM], F32, f"R{i}") for i in range(6)]
    RH0 = sb([128, GM], BF16, "RH0")
    trT = sb([128, 16, 128], BF16, "trT")
    S = sb([128, DHG], F32, "S")
    Sb = sb([128, DHG], BF16, "Sb")
    xpad = sb([128, GM + 3], F32, "xpad")
    hist = sb([128, 4, 3], F32, "hist")
    hstate = sb([128, 4], F32, "hstate")
    actb = [sb([128, SBJ, GM], BF16, f"actb{i}") for i in range(2)]
    lbc = sb([128, 2, DHG], F32, "lbc", dma=True)
    ghg = sb([128, DHG], F32, "ghg", dma=True)
    mask4 = sb([128, 128], BF16, "mask4")
    Mqs = sb([128, 128], F32, "Mqs")
    Mkd = sb([128, 128], F32, "Mkd")
    Mka = sb([128, 128], F32, "Mka")
    Mkb = sb([128, 128], F32, "Mkb")
    identf = sb([128, 128], F32, "identf")
    ident = sb([128, 128], BF16, "ident")
    ones = sb([128, 1], F32, "ones")
    rgv = sb([128, 36], F32, "rgv", dma=True)
    rgc = sb([128, 16], F32, "rgc")
    wgate = sb([128, 8, 128], BF16, "wgate", dma=True)
    sm_ns = [sb([128, 2], F32, f"sm_n{i}") for i in range(3)]
    sm_h = sb([128, 8], F32, "sm_h")
    sm_rs = [sb([128, 8], F32, f"sm_r{i}") for i in range(3)]
    sm_f = sb([128, 4], F32, "sm_f")
    eB = sb([128, 4], F32, "eB")

    PS = P.es.enter_context(nc.psum_tensor("PS", [128, 8 * 512], F32))
    banks = [Buf(PS, f"bank{i}") for i in range(8)]
    bk = lambda i: PS[:, 512 * i:512 * (i + 1)]
    bkb = lambda i: PS[:, 512 * i:512 * (i + 1)].bitcast(BF16)
    rot = {"r": 0, "h": 0, "f": 0, "p": 0}
    FFN_BANKS = (4, 5, 6)
    PP_BANKS = (0, 1, 2, 3)

    def nbr():
        rot["r"] = (rot["r"] + 1) % 2
        return rot["r"]

    def nbh():
        rot["h"] = (rot["h"] + 1) % 2
        return 2 + rot["h"]

    def nbp():
        rot["p"] = (rot["p"] + 1) % len(PP_BANKS)
        return PP_BANKS[rot["p"]]

    def nbf():
        rot["f"] = (rot["f"] + 1) % len(FFN_BANKS)
        return FFN_BANKS[rot["f"]]

    PIN = 7
    op = P.op

    for kc in range(KC):
        for c0 in (0, 1536):
            P.dma_load("pool", w_in, I("dma_start", out=w_in[:, kc, c0:c0 + 1536], in_=win_d[kc * 128:(kc + 1) * 128, c0:c0 + 1536]))
    for kc in range(KC):
        P.dma_load("pool", w_out, I("dma_start", out=w_out[:, kc, :], in_=wout_d[kc * 128:(kc + 1) * 128, :]))
    P.dma_load("pool", wgate, I("dma_start", out=wgate[:, :, :].rearrange("p a b -> p (a b)"), in_=wgate_d[:, :]))
    P.dma_load("sp", gfin, I("dma_start", out=gfin[:, :], in_=gvec_d[2:3, :].partition_broadcast(128)))
    P.dma_load("sp", gcol, I("dma_start", out=gcol[:, :], in_=gcol_d[:, :]))
    for i in range(2):
        P.dma_load("sp", lbc, I("dma_start", out=lbc[:, i, :], in_=hglb_d[i:i + 1, :].partition_broadcast(128)))
    for i in range(4):
        P.dma_load("sp", ghg, I("dma_start", out=ghg[:, i * 128:(i + 1) * 128], in_=hgg_d[0:1, :].partition_broadcast(128)))
    P.dma_load("sp", rgv, I("dma_start", out=rgv[:, :], in_=rgv_d[:, :]))

    op("pool", [], [identf], I("memset", identf[:, :], 1.0))
    op("pool", [identf], [identf], I("affine_select", out=identf[:, :], in_=identf[:, :], pattern=[[-1, 128]], compare_op=ALU.is_equal, fill=0.0, base=0, channel_multiplier=1))
    op("pool", [], [Mqs], I("memset", Mqs[:, :], 1.0))
    op("pool", [Mqs], [Mqs], I("affine_select", out=Mqs[:, :], in_=Mqs[:, :], pattern=[[1, 128]], compare_op=ALU.is_ge, fill=0.0, base=0, channel_multiplier=-1))
    op("pool", [], [Mkd], I("memset", Mkd[:, :], 1.0))
    op("pool", [Mkd], [Mkd], I("affine_select", out=Mkd[:, :], in_=Mkd[:, :], pattern=[[-1, 128]], compare_op=ALU.is_gt, fill=0.0, base=0, channel_multiplier=1))
    op("pool", [], [ones], I("memset", ones[:, :], 1.0))
    op("pool", [], [S], I("memset", S[:, :], 0.0))
    op("pool", [], [Sb], I("memset", Sb[:, :], 0.0))
    op("pool", [], [hist], I("memset", hist[:, :, :], 0.0))
    op("pool", [], [hstate], I("memset", hstate[:, :], 0.0))
    op("pool", [], [Mkb], I("memset", Mkb[:, :], 0.0))
    op("dve", [identf], [ident], I("tensor_copy", out=ident[:, :], in_=identf[:, :]))
    op("dve", [Mqs], [Mka], I("tensor_scalar", out=Mka[:, :], in0=Mqs[:, :], scalar1=-1.0, scalar2=None, op0=ALU.mult))
    op("dve", [Mka], [Mka], I("memset", Mka[0:64, 64:128], 0.0))
    op("dve", [Mkd, Mkb], [Mkb], I("tensor_copy", out=Mkb[0:64, 0:64], in_=Mkd[0:64, 0:64]))
    op("dve", [Mka, Mkb], [Mkb], I("tensor_copy", out=Mkb[64:128, 64:128], in_=Mka[64:128, 64:128]))
    op("dve", [Mqs], [mask4], I("tensor_copy", out=mask4[:, :], in_=Mqs[:, :]))
    op("dve", [rgv], [rgc], I("tensor_scalar", out=rgc[:, 0:8], in0=rgv[:, 20:28], scalar1=0.5, scalar2=None, op0=ALU.mult))
    op("act", [rgv], [rgc], I("activation", out=rgc[:, 8:12], in_=rgv[:, 28:32], func=AF.Exp, scale=-1.0))
    op("act", [rgc], [rgc], I("activation", out=rgc[:, 8:12], in_=rgc[:, 8:12], func=AF.Ln, bias=1.0))
    op("dve", [rgc], [rgc], I("tensor_scalar", out=rgc[:, 12:16], in0=rgc[:, 8:12], scalar1=-4.0, scalar2=None, op0=ALU.mult))
    op("dve", [rgc], [rgc], I("tensor_scalar", out=rgc[:, 8:12], in0=rgc[:, 8:12], scalar1=-8.0, scalar2=None, op0=ALU.mult))
    op("dve", [lbc], [lbc], I("tensor_tensor", out=lbc[:, 0, :], in0=lbc[:, 0, :], in1=lbc[:, 1, :], op=ALU.subtract))
    op("act", [lbc], [lbc], I("activation", out=lbc[:, 1, :], in_=lbc[:, 0, :], func=AF.Tanh, scale=0.5))
    op("dve", [lbc], [lbc], I("tensor_scalar", out=lbc[:, 0, :], in0=lbc[:, 1, :], scalar1=0.25, scalar2=0.75, op0=ALU.mult, op1=ALU.add))
    op("dve", [lbc], [lbc], I("tensor_scalar", out=lbc[:, 1, :], in0=lbc[:, 1, :], scalar1=-0.25, scalar2=0.25, op0=ALU.mult, op1=ALU.add))

    n_loads = len(groups) * NJ
    wgu_src = wgu_d.rearrange("(kc p) n -> p kc n", p=128)

    def emit_gu_load(st, L):
        if L >= n_loads:
            return
        j = L % NJ
        slot = gu_ring[L % NGU]
        wg, wu = slot["wg"], slot["wu"]
        st.dma_load("pool", wg, I("dma_start", out=wg[:, :, :], in_=wgu_src[:, :, j * 128:(j + 1) * 128]))
        st.dma_load("pool", wu, I("dma_start", out=wu[:, :, :], in_=wgu_src[:, :, DFF + j * 128:DFF + (j + 1) * 128]))

    def emit_wd_load(st, L):
        if L >= n_loads:
            return
        j = L % NJ
        wd = wd_ring[L % NWD]
        st.dma_load("pool", wd, I("dma_start", out=wd[:, :], in_=wdn_d[j * 128:(j + 1) * 128, :]))

    def rstd_from_ss(st, smx, ss_ap, out_ap, width):
        st.op("act", [smx], [smx], I("activation", out=out_ap, in_=ss_ap, func=AF.Ln, scale=1.0 / width, bias=EPS))
        st.op("act", [smx], [smx], I("activation", out=out_ap, in_=out_ap, func=AF.Exp, scale=-0.5))

    def norm_to_aT(st, hb, nt, c0, gi, aT, li):
        xn32, sm_n, b = xn32s[li], sm_ns[li], li
        st.op("act", [hb], [xn32, sm_n], I("activation", out=xn32[:nt, :], in_=hb[:nt, :], func=AF.Square, accum_out=sm_n[:nt, 0:1]))
        rstd_from_ss(st, sm_n, sm_n[:nt, 0:1], sm_n[:nt, 1:2], D)
        st.op("dve", [hb, sm_n], [xn32], I("tensor_scalar", out=xn32[:nt, :], in0=hb[:nt, :], scalar1=sm_n[:nt, 1:2], scalar2=None, op0=ALU.mult))
        for i in range(2):
            st.op("pe", [xn32, identf], [banks[b]], [I("transpose", out=bk(b)[:, j * 128:j * 128 + nt], in_=xn32[:nt, (4 * i + j) * 128:(4 * i + j + 1) * 128], identity=identf[:nt, :nt]) for j in range(4)])
            st.op("dve", [banks[b], gcol], [aT], I("tensor_tensor", out=aT[:, 4 * i:4 * i + 4, c0:c0 + nt], in0=bk(b).rearrange("p (a b) -> p a b", a=4)[:, :, 0:nt],
                                                    in1=gcol[:, gi * 8 + 4 * i:gi * 8 + 4 * i + 4].unsqueeze(2).to_broadcast([128, 4, nt]), op=ALU.mult))

    def mm_acc(out_ap, pairs):
        n = len(pairs)
        return [I("matmul", out_ap, lhsT=l, rhs=r, start=(i == 0), stop=(i == n - 1)) for i, (l, r) in enumerate(pairs)]

    def group_info(g):
        gtiles = [tiles[i] for i in groups[g]]
        gp0 = gtiles[0][0]
        N = sum(nt for _, nt in gtiles)
        chunks = [(0, N)] if N <= 512 else [(0, N // 2), (N // 2, N - N // 2)]
        return gtiles, gp0, N, chunks

    def gen_pre_tile(g, li, st):
        gtiles, gp0, N, chunks = group_info(g)
        hb = hsets[g % 2][li]
        aT = aTs[g % 2]
        p0, nt = gtiles[li]
        if p0 == 0:
            st.dma_load("sp", hb, I("dma_start", out=hb[0:NMETA, :], in_=meta_d[:, :]))
            st.dma_load("sp", hb, I("dma_start", out=hb[NMETA:128, :], in_=x_d[0:128 - NMETA, :]))
        else:
            st.dma_load("sp", hb, I("dma_start", out=hb[0:nt, :], in_=x_d[p0 - NMETA:p0 - NMETA + nt, :]))
        norm_to_aT(st, hb, nt, p0 - gp0, 0, aT, li)

    def gen_rg(g, st):
        gtiles, gp0, N, chunks = group_info(g)
        aT = aTs[g % 2]
        nbm = nbr
        F0, F1, F2, F3, F4, F5 = Rt
        H0 = RH0
        for blk in range(4):
            cw = [rgv[:, 4 * j + blk:4 * j + blk + 1] for j in range(4)]
            cb = rgv[:, 16 + blk:17 + blk]
            pX = [nbm() for _ in chunks]
            for ci, (cs, cn) in enumerate(chunks):
                col = blk * 128
                st.op("pe", [w_in, aT], [banks[pX[ci]]], mm_acc(bk(pX[ci])[:, 0:cn], [(w_in[:, kc, col:col + 128], aT[:, kc, cs:cs + cn]) for kc in range(KC)]))
            st.op("dve", [hist], [xpad], I("tensor_copy", out=xpad[:, 0:3], in_=hist[:, blk, :]))
            for ci, (cs, cn) in enumerate(chunks):
                st.op("act", [banks[pX[ci]]], [xpad], I("activation", out=xpad[:, 3 + cs:3 + cs + cn], in_=bk(pX[ci])[:, 0:cn], func=AF.Copy))
            pG = [nbm() for _ in chunks]
            for ci, (cs, cn) in enumerate(chunks):
                col = DRG + blk * 128
                st.op("pe", [w_in, aT], [banks[pG[ci]]], mm_acc(bk(pG[ci])[:, 0:cn], [(w_in[:, kc, col:col + 128], aT[:, kc, cs:cs + cn]) for kc in range(KC)]))
                st.op("act", [banks[pG[ci]]], [F5], I("activation", out=F5[:, cs:cs + cn], in_=bk(pG[ci])[:, 0:cn], func=AF.Copy))
            st.op("dve", [xpad, rgv], [F0], I("tensor_scalar", out=F0[:, 0:N], in0=xpad[:, 3:3 + N], scalar1=cw[3], scalar2=cb, op0=ALU.mult, op1=ALU.add))
            for j in (2, 1, 0):
                st.op("dve", [xpad, rgv, F0], [F0], I("scalar_tensor_tensor", out=F0[:, 0:N], in0=xpad[:, j:j + N], scalar=cw[j], in1=F0[:, 0:N], op0=ALU.mult, op1=ALU.add))
            st.op("dve", [xpad], [hist], I("tensor_copy", out=hist[:, blk, :], in_=xpad[:, N:N + 3]))
            st.op("act", [F0], [H0], I("activation", out=H0[:, 0:N], in_=F0[:, 0:N], func=AF.Copy))
            st.cut()
            for ci, (cs, cn) in enumerate(chunks):
                pR = nbm()
                st.op("pe", [wgate, H0], [banks[pR]], I("matmul", bk(pR)[:, 0:cn], lhsT=wgate[:, blk, :], rhs=H0[:, cs:cs + cn], start=True, stop=True))
                st.op("act", [banks[pR], rgc], [F1], I("activation", out=F1[:, cs:cs + cn], in_=bk(pR)[:, 0:cn], func=AF.Tanh, scale=0.5, bias=rgc[:, blk:blk + 1]))
                pI = nbm()
                st.op("pe", [wgate, H0], [banks[pI]], I("matmul", bk(pI)[:, 0:cn], lhsT=wgate[:, 4 + blk, :], rhs=H0[:, cs:cs + cn], start=True, stop=True))
                st.op("act", [banks[pI], rgc], [F2], I("activation", out=F2[:, cs:cs + cn], in_=bk(pI)[:, 0:cn], func=AF.Tanh, scale=0.5, bias=rgc[:, 4 + blk:5 + blk]))
            st.op("act", [F5], [F4], I("activation", out=F4[:, 0:N], in_=F5[:, 0:N], func=AF.Square))
            st.op("dve", [F4], [F4], I("tensor_scalar", out=F4[:, 0:N], in0=F4[:, 0:N], scalar1=0.044715, scalar2=1.0, op0=ALU.mult, op1=ALU.add))
            st.op("dve", [F4, F5], [F4], I("tensor_tensor", out=F4[:, 0:N], in0=F4[:, 0:N], in1=F5[:, 0:N], op=ALU.mult))
            st.op("act", [F4], [F4], I("activation", out=F4[:, 0:N], in_=F4[:, 0:N], func=AF.Tanh, scale=GELU_C))
            hcl = rgc[:, 12 + blk:13 + blk]
            cl = rgc[:, 8 + blk:9 + blk]
            st.op("act", [F1, rgc], [F3], I("activation", out=F3[:, 0:N], in_=F1[:, 0:N], func=AF.Exp, scale=hcl, bias=hcl))
            st.op("act", [F1, rgc], [F1], I("activation", out=F1[:, 0:N], in_=F1[:, 0:N], func=AF.Exp, scale=cl, bias=cl))
            st.op("act", [F1], [F1], I("activation", out=F1[:, 0:N], in_=F1[:, 0:N], func=AF.Ln, scale=-1.0, bias=1.0))
            st.op("act", [F1], [F1], I("activation", out=F1[:, 0:N], in_=F1[:, 0:N], func=AF.Exp, scale=0.5))
            st.op("dve", [F2, F0], [F2], I("scalar_tensor_tensor", out=F2[:, 0:N], in0=F2[:, 0:N], scalar=1.0, in1=F0[:, 0:N], op0=ALU.add, op1=ALU.mult))
            st.op("dve", [F2, F1], [F2], I("scalar_tensor_tensor", out=F2[:, 0:N], in0=F2[:, 0:N], scalar=0.5, in1=F1[:, 0:N], op0=ALU.mult, op1=ALU.mult))
            st.op("dve", [F3, F2, hstate], [F0], I("tensor_tensor_scan", out=F0[:, 0:N], data0=F3[:, 0:N], data1=F2[:, 0:N], initial=hstate[:, blk:blk + 1], op0=ALU.mult, op1=ALU.add))
            st.op("dve", [F0], [hstate], I("tensor_copy", out=hstate[:, blk:blk + 1], in_=F0[:, N - 1:N]))
            st.op("dve", [F4, F5], [F4], I("scalar_tensor_tensor", out=F4[:, 0:N], in0=F4[:, 0:N], scalar=1.0, in1=F5[:, 0:N], op0=ALU.add, op1=ALU.mult))
            st.op("dve", [F4, F0], [F4], I("scalar_tensor_tensor", out=F4[:, 0:N], in0=F4[:, 0:N], scalar=0.5, in1=F0[:, 0:N], op0=ALU.mult, op1=ALU.mult))
            st.op("dve", [F4, rgv], [yT], I("tensor_scalar", out=yT[:, blk, 0:N], in0=F4[:, 0:N], scalar1=rgv[:, 32 + blk:33 + blk], scalar2=None, op0=ALU.mult))
            st.op("act", [F4], [F2], I("activation", out=F2[:, 0:N], in_=F4[:, 0:N], func=AF.Square))
            st.cut()
            st.op("pe", [F2, ones], [banks[PIN]], [I("matmul", bk(PIN)[:nt, li * 4 + blk:li * 4 + blk + 1], lhsT=F2[:, p0 - gp0:p0 - gp0 + nt], rhs=ones[:, 0:1], start=True, stop=True) for li, (p0, nt) in enumerate(gtiles)])

    def gen_hg(g, st):
        gtiles, gp0, N, chunks = group_info(g)
        aT = aTs[g % 2]
        nbm = nbh
        for li, (p0, nt) in enumerate(gtiles):
            c0 = p0 - gp0
            fK, fLF, fQ, fG, fE0, fE1 = Ft
            hV, hKA, hQB, hKB, hQS, hKD, hAT = Ht
            st.cut()
            pf = nbm()
            st.op("pe", [aT, w_in], [banks[pf]], mm_acc(bk(pf)[:nt, :], [(aT[:, kc, c0:c0 + nt], w_in[:, kc, 1536:2048]) for kc in range(KC)]))
            st.op("act", [banks[pf]], [fK], I("activation", out=fK[:nt, 0:512], in_=bk(pf)[:nt, :], func=AF.Tanh, scale=0.5))
            pq = nbm()
            st.op("pe", [aT, w_in], [banks[pq]], mm_acc(bk(pq)[:nt, :], [(aT[:, kc, c0:c0 + nt], w_in[:, kc, 1024:1536]) for kc in range(KC)]))
            st.op("act", [banks[pq]], [fQ], I("activation", out=fQ[:nt, 0:512], in_=bk(pq)[:nt, :], func=AF.Silu))
            pg = nbm()
            st.op("pe", [aT, w_in], [banks[pg]], mm_acc(bk(pg)[:nt, :], [(aT[:, kc, c0:c0 + nt], w_in[:, kc, 2560:3072]) for kc in range(KC)]))
            st.op("act", [banks[pg]], [fG], I("activation", out=fG[:nt, 0:512], in_=bk(pg)[:nt, :], func=AF.Silu))
            pi_ = nbm()
            st.op("pe", [aT, w_in], [banks[pi_]], mm_acc(bk(pi_)[:nt, :], [(aT[:, kc, c0:c0 + nt], w_in[:, kc, 2048:2560]) for kc in range(KC)]))
            st.op("act", [banks[pi_]], [hV], I("activation", out=hV[:nt, 0:512], in_=bk(pi_)[:nt, :], func=AF.Copy))
            st.op("dve", [fK, lbc], [fK], I("tensor_tensor", out=fK[:nt, 0:512], in0=fK[:nt, 0:512], in1=lbc[:nt, 1, :], op=ALU.mult))
            st.op("dve", [fK, lbc], [fK], I("tensor_tensor", out=fK[:nt, 0:512], in0=fK[:nt, 0:512], in1=lbc[:nt, 0, :], op=ALU.add))
            st.op("act", [fK], [fLF], I("activation", out=fLF[:nt, 0:512], in_=fK[:nt, 0:512], func=AF.Ln))
            st.op("dve", [fK], [fK], I("tensor_scalar", out=fK[:nt, 0:512], in0=fK[:nt, 0:512], scalar1=-1.0, scalar2=1.0, op0=ALU.mult, op1=ALU.add))
            st.op("dve", [fG, ghg], [fG], I("tensor_tensor", out=fG[:nt, 0:512], in0=fG[:nt, 0:512], in1=ghg[:nt, :], op=ALU.mult))
            st.cut()
            st.op("pe", [fLF, ones], [banks[PIN]], [I("matmul", bk(PIN)[:, 32 + hh:33 + hh], lhsT=fLF[:nt, hh * 128:(hh + 1) * 128], rhs=ones[:nt, 0:1], start=True, stop=True) for hh in range(4)])
            st.op("act", [banks[PIN]], [eB], I("activation", out=eB[:, 0:4], in_=bk(PIN)[:, 32:36], func=AF.Exp))

            def expmul(M, scales_srcs_dsts):
                b = nbm()
                st.op("pe", [M, fLF], [banks[b]], I("matmul", bk(b)[:nt, :], lhsT=M[:nt, :nt], rhs=fLF[:nt, 0:512], start=True, stop=True))
                for scale, src, dst, tmp in scales_srcs_dsts:
                    st.op("act", [banks[b]], [tmp], I("activation", out=tmp[:nt, 0:512], in_=bk(b)[:nt, :], func=AF.Exp, scale=scale))
                    st.op("dve", [tmp, src], [dst], I("tensor_tensor", out=dst[:nt, 0:512], in0=src[:nt, 0:512], in1=tmp[:nt, 0:512], op=ALU.mult))
            expmul(Mka, [(1.0, fK, hKA, fE0), (-1.0, fQ, hQB, fE1)])
            expmul(Mkb, [(1.0, fK, hKB, fE0)])
            expmul(Mqs, [(1.0, fQ, hQS, fE1)])
            expmul(Mkd, [(1.0, fK, hKD, fE0)])
            st.cut()
            for half, srcs in ((0, (hQB, hQS)), (1, (hKA, hKB))):
                b = nbm()
                st.op("pe", [srcs[0], srcs[1], ident], [banks[b]], [I("transpose", out=bkb(b)[:, (vi * 4 + hh) * 128:(vi * 4 + hh) * 128 + nt], in_=s_[:nt, hh * 128:(hh + 1) * 128], identity=ident[:nt, :nt]) for vi, s_ in enumerate(srcs) for hh in range(4)])
                dst = trT[:, half * 8:half * 8 + 8, 0:nt]
                src = bkb(b).rearrange("p (a b) -> p a b", a=8)[:, :, 0:nt]
                if half == 0:
                    st.op("act", [banks[b]], [trT], I("activation", out=dst, in_=src, func=AF.Copy))
                else:
                    st.op("dve", [banks[b]], [trT], I("tensor_copy", out=dst, in_=src))
            st.cut()
            pA = nbm()
            n0 = min(64, nt)
            insA = []
            for hh in range(4):
                insA.append(I("matmul", bk(pA)[:nt, hh * 128:hh * 128 + n0], lhsT=trT[:, 8 + hh, 0:nt], rhs=trT[:, 0 + hh, 0:n0], start=True, stop=True))
                if nt > 64:
                    insA.append(I("matmul", bk(pA)[:nt, hh * 128 + 64:hh * 128 + nt], lhsT=trT[:, 12 + hh, 0:nt], rhs=trT[:, 0 + hh, 64:nt], start=True, stop=True))
            st.op("pe", [trT], [banks[pA]], insA)
            pU = nbm()
            st.op("pe", [hKD, hV], [banks[pU]], [I("matmul", bk(pU)[:, hh * 128:(hh + 1) * 128], lhsT=hKD[:nt, hh * 128:(hh + 1) * 128], rhs=hV[:nt, hh * 128:(hh + 1) * 128], start=True, stop=True) for hh in range(4)])
            if nt == 128:
                st.op("dve", [banks[pA], mask4], [hAT], I("tensor_tensor", out=hAT[:nt, 0:512].rearrange("p (a b) -> p a b", a=4), in0=bk(pA)[:nt, :].rearrange("p (a b) -> p a b", a=4), in1=mask4[:nt, :].unsqueeze(1).to_broadcast([nt, 4, 128]), op=ALU.mult))
            else:
                for hh in range(4):
                    st.op("dve", [banks[pA], mask4], [hAT], I("tensor_tensor", out=hAT[:nt, hh * 128:hh * 128 + nt], in0=bk(pA)[:nt, hh * 128:hh * 128 + nt], in1=mask4[:nt, 0:nt], op=ALU.mult))
            st.cut()
            pO = nbm()
            insO = []
            for hh in range(4):
                hs_ = slice(hh * 128, (hh + 1) * 128)
                insO.append(I("matmul", bk(pO)[:nt, hs_], lhsT=hAT[:nt, hh * 128:hh * 128 + nt], rhs=hV[:nt, hs_], start=True, stop=False))
                insO.append(I("matmul", bk(pO)[:nt, hs_], lhsT=trT[:, 4 + hh, 0:nt], rhs=Sb[:, hs_], start=False, stop=True))
            st.op("pe", [hAT, hV, trT, Sb], [banks[pO]], insO)
            for hh in range(4):
                hs_ = slice(hh * 128, (hh + 1) * 128)
                st.op("dve", [S, eB, banks[pU]], [S], I("scalar_tensor_tensor", out=S[:, hs_], in0=S[:, hs_], scalar=eB[:, hh:hh + 1], in1=bk(pU)[:, hs_], op0=ALU.mult, op1=ALU.add))
            st.op("act", [S], [Sb], I("activation", out=Sb[:, :], in_=S[:, :], func=AF.Copy))
            for hh in range(4):
                hs_ = slice(hh * 128, (hh + 1) * 128)
                st.op("act", [banks[pO]], [fE0, sm_h], I("activation", out=fE0[:nt, hs_], in_=bk(pO)[:nt, hs_], func=AF.Square, accum_out=sm_h[:nt, hh:hh + 1]))
            rstd_from_ss(st, sm_h, sm_h[:nt, 0:4], sm_h[:nt, 4:8], 128)
            st.op("dve", [banks[pO], sm_h], [fE1], I("tensor_tensor", out=fE1[:nt, 0:512].rearrange("p (a b) -> p a b", a=4), in0=bk(pO)[:nt, :].rearrange("p (a b) -> p a b", a=4), in1=sm_h[:nt, 4:8].unsqueeze(2).to_broadcast([nt, 4, 128]), op=ALU.mult))
            st.op("dve", [fE1, fG], [hKA], I("tensor_tensor", out=hKA[:nt, 0:512], in0=fE1[:nt, 0:512], in1=fG[:nt, 0:512], op=ALU.mult))
            st.cut()
            b = nbm()
            st.op("pe", [hKA, ident], [banks[b]], [I("transpose", out=bkb(b)[:, hh * 128:hh * 128 + nt], in_=hKA[:nt, hh * 128:(hh + 1) * 128], identity=ident[:nt, :nt]) for hh in range(4)])
            st.op("act", [banks[b]], [yT], I("activation", out=yT[:, 4:8, c0:c0 + nt], in_=bkb(b)[:, 0:512].rearrange("p (a b) -> p a b", a=4)[:, :, 0:nt], func=AF.Copy))

    def gen_post_tile(g, li, st):
        gtiles, gp0, N, chunks = group_info(g)
        hb = hsets[g % 2][li]
        aT = aTs[g % 2]
        sm_r = sm_rs[li]
        p0, nt = gtiles[li]
        c0 = p0 - gp0
        b = li
        st.op("dve", [banks[PIN]], [sm_r], I("tensor_copy", out=sm_r[:nt, 0:4], in_=bk(PIN)[:nt, li * 4:li * 4 + 4]))
        st.op("dve", [sm_r], [sm_r], I("tensor_tensor", out=sm_r[:nt, 4:6], in0=sm_r[:nt, 0:2], in1=sm_r[:nt, 2:4], op=ALU.add))
        st.op("dve", [sm_r], [sm_r], I("tensor_tensor", out=sm_r[:nt, 6:7], in0=sm_r[:nt, 4:5], in1=sm_r[:nt, 5:6], op=ALU.add))
        rstd_from_ss(st, sm_r, sm_r[:nt, 6:7], sm_r[:nt, 7:8], DRG)
        for half in range(2):
            cs_ = slice(half * 512, (half + 1) * 512)
            st.op("pe", [yT, w_out], [banks[b]], mm_acc(bk(b)[:nt, :], [(yT[:, blk, c0:c0 + nt], w_out[:, blk, cs_]) for blk in range(4)]))
            st.op("dve", [banks[b], sm_r, hb], [hb], I("scalar_tensor_tensor", out=hb[:nt, cs_], in0=bk(b)[:nt, :], scalar=sm_r[:nt, 7:8], in1=hb[:nt, cs_], op0=ALU.mult, op1=ALU.add))
            st.op("pe", [yT, w_out], [banks[b]], mm_acc(bk(b)[:nt, :], [(yT[:, blk, c0:c0 + nt], w_out[:, blk, cs_]) for blk in range(4, 8)]))
            st.op("dve", [banks[b], hb], [hb], I("tensor_tensor", out=hb[:nt, cs_], in0=bk(b)[:nt, :], in1=hb[:nt, cs_], op=ALU.add))
        norm_to_aT(st, hb, nt, c0, 1, aT, li)

    def gen_ffn(g, st):
        gtiles, gp0, N, chunks = group_info(g)
        hbuf = hsets[g % 2]
        aT = aTs[g % 2]
        j0 = 0
        sbi = 0
        while j0 < NJ:
            nj = min(SBJ, NJ - j0)
            ab = actb[sbi % 2]
            for jj in range(nj):
                L = g * NJ + j0 + jj
                slot = gu_ring[L % NGU]
                for ci, (cs, cn) in enumerate(chunks):
                    pg_, pu_ = nbf(), nbf()
                    st.op("pe", [slot["wg"], aT], [banks[pg_]], mm_acc(bk(pg_)[:, 0:cn], [(slot["wg"][:, kc, :], aT[:, kc, cs:cs + cn]) for kc in range(KC)]))
                    st.op("pe", [slot["wu"], aT], [banks[pu_]], mm_acc(bk(pu_)[:, 0:cn], [(slot["wu"][:, kc, :], aT[:, kc, cs:cs + cn]) for kc in range(KC)]))
                    tmp = Gt[jj % 2]
                    st.op("act", [banks[pg_]], [tmp], I("activation", out=tmp[:, cs:cs + cn], in_=bk(pg_)[:, 0:cn], func=AF.Silu))
                    st.op("dve", [tmp, banks[pu_]], [ab], I("tensor_tensor", out=ab[:, jj, cs:cs + cn], in0=tmp[:, cs:cs + cn], in1=bk(pu_)[:, 0:cn], op=ALU.mult))
                emit_gu_load(st, L + NGU)
            wds = [wd_ring[(g * NJ + j0 + jj) % NWD] for jj in range(nj)]
            for li, (p0, nt) in enumerate(gtiles):
                c0 = p0 - gp0
                hb = hbuf[li]
                for half in range(2):
                    pd = nbf()
                    cs_ = slice(half * 512, (half + 1) * 512)
                    st.op("pe", [ab] + wds, [banks[pd]], mm_acc(bk(pd)[:nt, :], [(ab[:, jj, c0:c0 + nt], wds[jj][:, cs_]) for jj in range(nj)]))
                    st.op("dve", [banks[pd], hb], [hb], I("tensor_tensor", out=hb[:nt, cs_], in0=bk(pd)[:nt, :], in1=hb[:nt, cs_], op=ALU.add))
            for jj in range(nj):
                emit_wd_load(st, g * NJ + j0 + jj + NWD)
            j0 += nj
            sbi += 1
        for li, (p0, nt) in enumerate(gtiles):
            hb = hbuf[li]
            for half in range(2):
                jb = nbf()
                st.op("act", [hb], [banks[jb], sm_f], I("activation", out=bk(jb)[:nt, :], in_=hb[:nt, half * 512:(half + 1) * 512], func=AF.Square, accum_out=sm_f[:nt, 2 + half:3 + half]))
            st.op("dve", [sm_f], [sm_f], I("tensor_tensor", out=sm_f[:nt, 0:1], in0=sm_f[:nt, 2:3], in1=sm_f[:nt, 3:4], op=ALU.add))
            rstd_from_ss(st, sm_f, sm_f[:nt, 0:1], sm_f[:nt, 1:2], D)
            st.op("dve", [hb, sm_f, gfin], [hb], I("scalar_tensor_tensor", out=hb[:nt, :], in0=hb[:nt, :], scalar=sm_f[:nt, 1:2], in1=gfin[:nt, :], op0=ALU.mult, op1=ALU.mult))
            if p0 == 0:
                st.dma_store("sp", hb, I("dma_start", out=out_d[0:128 - NMETA, :], in_=hb[NMETA:128, :]))
            else:
                st.dma_store("sp", hb, I("dma_start", out=out_d[p0 - NMETA:p0 - NMETA + nt, :], in_=hb[0:nt, :]))
            st.cut()

    def flat(segs):
        return [it for seg in segs for it in seg]

    S_SET = (AF.Silu, AF.Tanh)
    L_SET = (AF.Exp, AF.Ln)

    def act_set(item):
        kind, a_ = item
        if kind != "op" or a_[0] != "act":
            return None
        insts = a_[3]
        if isinstance(insts, tuple):
            insts = [insts]
        f = insts[-1][2].get("func")
        if f in S_SET:
            return "S"
        if f in L_SET:
            return "L"
        return None

    ZIP_TOL = 0.03

    def zip_n(lists, tol=None):
        tol = ZIP_TOL if tol is None else tol
        lists = [l for l in lists if l]
        out = []
        idx = [0] * len(lists)
        cur = None
        while True:
            live = [k for k in range(len(lists)) if idx[k] < len(lists[k])]
            if not live:
                break
            frac = {k: (idx[k] + 1) / len(lists[k]) for k in live}
            fmin = min(frac.values())
            cands = [k for k in live if frac[k] <= fmin + tol]

            def cost(k):
                st_ = act_set(lists[k][idx[k]])
                return (1 if (st_ is not None and cur is not None and st_ != cur) else 0, frac[k])
            k = min(cands, key=cost)
            it = lists[k][idx[k]]
            st_ = act_set(it)
            if st_ is not None:
                cur = st_
            out.append(it)
            idx[k] += 1
        return out

    def zip_items(A, B):
        return zip_n([A, B])

    def mixer_items(g):
        if PROBE_SKIP_MIXER:
            return []
        ntile = len(groups[g])
        pres, posts = [], []
        for li in range(ntile):
            a_, b_ = Stream(), Stream()
            gen_pre_tile(g, li, a_)
            gen_post_tile(g, li, b_)
            pres.append(flat(a_.segs))
            posts.append(flat(b_.segs))
        rg, hg = Stream(), Stream()
        gen_rg(g, rg)
        gen_hg(g, hg)
        return zip_n(pres) + zip_items(flat(rg.segs), flat(hg.segs)) + zip_n(posts)

    NG = len(groups)
    st0 = Stream()
    for L0 in range(NGU):
        emit_gu_load(st0, L0)
    for L0 in range(NWD):
        emit_wd_load(st0, L0)
    flush_segment(P, flat(st0.segs) + mixer_items(0))
    for g in range(NG):
        sf = Stream()
        if not PROBE_SKIP_FFN:
            gen_ffn(g, sf)
        mit = mixer_items(g + 1) if g + 1 < NG else []
        flush_segment(P, zip_items(flat(sf.segs), mit))
    P.final_wait("sp", hsets[0] + hsets[1])
    P.emit()
    P.close()
    return nc


def _layout_inputs(inp):
    f = lambda a: np.ascontiguousarray(np.asarray(a, dtype=np.float32))
    cw = f(inp["conv_w"])[0]
    cols = []
    for j in range(4):
        cols.append(cw[j].reshape(4, 128).T)
    for nm in ("conv_b", "b_rgate", "b_igate", "lru_lambda", "rg_norm_g"):
        cols.append(f(inp[nm])[0].reshape(4, 128).T)
    rgv = f(np.concatenate(cols, axis=1))
    wg = np.zeros((128, 8, 128), np.float32)
    for gi, nm in enumerate(("w_rgate", "w_igate")):
        w = f(inp[nm])[0]
        for blk in range(4):
            wg[0:64, gi * 4 + blk, 0:64] = w[2 * blk]
            wg[64:128, gi * 4 + blk, 64:128] = w[2 * blk + 1]
    shared = {
        "meta": f(inp["meta_tokens"]),
        "w_in": f(inp["w_in"])[0],
        "w_out": f(inp["w_out"])[0],
        "w_gu": f(inp["w_gate_up"])[0],
        "w_down": f(inp["w_down"])[0],
        "gvec": f(np.stack([f(inp["mix_norm_g"])[0], f(inp["ffn_norm_g"])[0], f(inp["final_norm_g"])], 0)),
        "rgv": rgv,
        "gcol": f(np.concatenate([f(inp["mix_norm_g"])[0].reshape(8, 128).T, f(inp["ffn_norm_g"])[0].reshape(8, 128).T], axis=1)),
        "wgate": f(wg.reshape(128, 8 * 128)),
        "hglb": f(inp["hg_lower_bound"]),
        "hgg": f(inp["hg_norm_g"]),
    }
    x = f(inp["x"])
    return [dict(shared, x=x[c]) for c in range(NCORES)]


def kernel(**inputs):
    in_maps = _layout_inputs(inputs)
    nc = build_nc()
    res = run_bass_kernel_spmd(nc, in_maps, core_ids=list(range(NCORES)))
    return np.stack([np.asarray(r["out"], dtype=np.float32) for r in res.results], axis=0)
```

```python
from contextlib import ExitStack
import numpy as np
import concourse.bass as bass
import concourse.mybir as mybir
from concourse.bass_utils import run_bass_kernel_spmd

F32 = mybir.dt.float32
BF16 = mybir.dt.bfloat16
AF = mybir.ActivationFunctionType
ALU = mybir.AluOpType

D = 1024
KC = 8
SEQ = 4096
NMETA = 16
T = SEQ + NMETA
DRG = 512
DHG = 512
DIN = 3072
DFF = 2816
NJ = DFF // 128
EPS = 1e-6
GELU_C = 0.7978845608028654
NSLOT = 4
SBJ = 6
NCORES = 8
PROBE_NOLOAD = False
PROBE_SKIP_FFN = False
PROBE_SKIP_MIXER = False


class Buf:
    def __init__(self, t, name):
        self.t = t
        self.name = name
        self.last_w = None
        self.reads = {}
        self.dma_sem = None
        self.dma_n = 0

    def __getitem__(self, k):
        return self.t[k]


class Eng:
    def __init__(self, name):
        self.name = name
        self.sem = None
        self.n = 0
        self.seen = {}
        self.prog = []


class Prog:
    def __init__(self, nc):
        self.nc = nc
        self.es = ExitStack()
        self.engs = {k: Eng(k) for k in ("pe", "act", "dve", "pool", "sp")}
        for k in ("pe", "act", "dve", "pool"):
            self.engs[k].sem = self.es.enter_context(nc.semaphore("s_" + k))

    def sbuf(self, shape, dtype, name, dma=False):
        t = self.es.enter_context(self.nc.sbuf_tensor("sb_" + name, list(shape), dtype))
        b = Buf(t, name)
        if dma:
            b.dma_sem = self.es.enter_context(self.nc.semaphore("d_" + name))
        return b

    def psum(self, shape, dtype, name):
        t = self.es.enter_context(self.nc.psum_tensor(name, list(shape), dtype))
        return Buf(t, name)

    def _wait(self, eng, tok, own_ok=False):
        if tok is None:
            return
        sem, val = tok
        if eng.sem is not None and sem is eng.sem and not own_ok:
            return
        key = id(sem)
        if eng.seen.get(key, 0) >= val:
            return
        eng.seen[key] = val
        eng.prog.append(("wait", sem, val))

    def op(self, ename, reads, writes, fn):
        eng = self.engs[ename]
        raw_own = ename in ("act", "dve", "pool")
        for b in reads:
            self._wait(eng, b.last_w, own_ok=raw_own)
        for b in writes:
            self._wait(eng, b.last_w)
            for sem, val in b.reads.values():
                self._wait(eng, (sem, val))
        eng.n += 1
        tok = (eng.sem, eng.n)
        eng.prog.append(("inst", fn, eng.sem, 1))
        for b in writes:
            b.last_w = tok
            b.reads = {}
        for b in reads:
            if b not in writes:
                b.reads[id(eng.sem)] = tok
        return tok

    def dma_load(self, qname, buf, fn):
        eng = self.engs[qname]
        lw = buf.last_w
        if lw is not None and lw[0] is not buf.dma_sem:
            self._wait(eng, lw)
        for sem, val in buf.reads.values():
            self._wait(eng, (sem, val))
        buf.dma_n += 1
        tok = (buf.dma_sem, 16 * buf.dma_n)
        eng.prog.append(("inst", fn, buf.dma_sem, 16))
        buf.last_w = tok
        buf.reads = {}

    def dma_store(self, qname, buf, fn):
        eng = self.engs[qname]
        self._wait(eng, buf.last_w)
        buf.dma_n += 1
        tok = (buf.dma_sem, 16 * buf.dma_n)
        eng.prog.append(("inst", fn, buf.dma_sem, 16))
        buf.reads[id(buf.dma_sem)] = tok

    def final_wait(self, qname, bufs):
        eng = self.engs[qname]
        for b in bufs:
            for sem, val in b.reads.values():
                self._wait(eng, (sem, val))
            self._wait(eng, b.last_w)

    def emit(self):
        engs = self.engs

        def replay(e, h):
            for it in e.prog:
                if it[0] == "wait":
                    h.wait_ge(it[1], it[2])
                else:
                    lst = it[1]
                    if isinstance(lst, tuple):
                        lst = [lst]
                    for (m, a, k) in lst:
                        ins = getattr(h, m)(*a, **k)
                    ins.then_inc(it[2], it[3])

        with self.nc.Block() as block:
            @block.tensor
            def _(h):
                replay(engs["pe"], h)

            @block.scalar
            def _(h):
                replay(engs["act"], h)

            @block.vector
            def _(h):
                replay(engs["dve"], h)

            @block.gpsimd
            def _(h):
                replay(engs["pool"], h)

            @block.sync
            def _(h):
                replay(engs["sp"], h)

    def close(self):
        self.es.close()


def I(m, *a, **k):
    return (m, a, k)


class Stream:
    def __init__(self):
        self.segs = [[]]

    def op(self, *a):
        self.segs[-1].append(("op", a))

    def dma_load(self, *a):
        self.segs[-1].append(("dma_load", a))

    def dma_store(self, *a):
        self.segs[-1].append(("dma_store", a))

    def cut(self):
        if self.segs[-1]:
            self.segs.append([])


def flush_segment(P, seg):
    for kind, a in seg:
        getattr(P, kind)(*a)


def merge_streams(P, A, B):
    sa = [x for x in A.segs if x]
    sb_ = [x for x in B.segs if x]
    ia = ib = 0
    while ia < len(sa) or ib < len(sb_):
        fa = (ia + 1) / len(sa) if ia < len(sa) else 2.0
        fb = (ib + 1) / len(sb_) if ib < len(sb_) else 2.0
        if fa <= fb:
            flush_segment(P, sa[ia])
            ia += 1
        else:
            flush_segment(P, sb_[ib])
            ib += 1


def tile_plan():
    tiles = [(128 * i, 128) for i in range(32)] + [(4096, 16)]
    groups = [list(range(3 * g, 3 * g + 3)) for g in range(11)]
    return tiles, groups


def build_nc(n_groups=None):
    nc = bass.Bass("TRN2", target_bir_lowering=False)
    dt = lambda name, shape, kind="ExternalInput": nc.dram_tensor(name, list(shape), F32, kind=kind).ap()
    x_d = dt("x", [SEQ, D])
    meta_d = dt("meta", [NMETA, D])
    win_d = dt("w_in", [D, DIN])
    wout_d = dt("w_out", [D, D])
    wgu_d = dt("w_gu", [D, 2 * DFF])
    wdn_d = dt("w_down", [DFF, D])
    gvec_d = dt("gvec", [3, D])
    gcol_d = dt("gcol", [128, 16])
    rgv_d = dt("rgv", [128, 36])
    wgate_d = dt("wgate", [128, 8 * 128])
    hglb_d = dt("hglb", [2, DHG])
    hgg_d = dt("hgg", [1, 128])
    out_d = dt("out", [SEQ, D], kind="ExternalOutput")

    tiles, groups = tile_plan()
    if n_groups is not None:
        groups = groups[:n_groups]

    P = Prog(nc)
    sb = P.sbuf
    w_in = sb([128, KC, DIN], BF16, "w_in", dma=True)
    w_out = sb([128, KC, D], BF16, "w_out", dma=True)
    NGU, NWD = 3, 7
    gu_ring = [dict(wg=sb([128, KC, 128], BF16, f"wg{i}", dma=True), wu=sb([128, KC, 128], BF16, f"wu{i}", dma=True)) for i in range(NGU)]
    wd_ring = [sb([128, D], BF16, f"wd{i}", dma=True) for i in range(NWD)]
    GM = 384
    hsets = [[sb([128, D], F32, f"hA{i}", dma=True) for i in range(3)], [sb([128, D], F32, f"hB{i}", dma=True) for i in range(3)]]
    aTs = [sb([128, KC, GM], BF16, "aT0"), sb([128, KC, GM], BF16, "aT1")]
    yT = sb([128, KC, GM], BF16, "yT")
    xn32s = [sb([128, D], F32, f"xn32_{i}") for i in range(3)]
    gfin = sb([128, D], F32, "gfin", dma=True)
    gcol = sb([128, 16], F32, "gcol", dma=True)
    Gt = [sb([128, GM], F32, f"G{i}") for i in range(2)]
    Ft = [sb([128, 512], F32, f"F{i}") for i in range(6)]
    Ht = [sb([128, 512], BF16, f"H{i}") for i in range(7)]
    Rt = [sb([128, GM], F32, f"R{i}") for i in range(6)]
    RH0 = sb([128, GM], BF16, "RH0")
    trT = sb([128, 16, 128], BF16, "trT")
    S = sb([128, DHG], F32, "S")
    Sb = sb([128, DHG], BF16, "Sb")
    xpad = sb([128, GM + 3], F32, "xpad")
    hist = sb([128, 4, 3], F32, "hist")
    hstate = sb([128, 4], F32, "hstate")
    actb = [sb([128, SBJ, GM], BF16, f"actb{i}") for i in range(2)]
    lbc = sb([128, DHG], F32, "lbc", dma=True)
    lb1 = Ft[5]
    lb1.dma_sem = P.es.enter_context(nc.semaphore("d_lb1"))
    ghg = sb([128, DHG], F32, "ghg", dma=True)
    mask4 = sb([128, 128], BF16, "mask4")
    Mqs = sb([128, 128], F32, "Mqs")
    Mkd = sb([128, 128], F32, "Mkd")
    Mka = sb([128, 128], F32, "Mka")
    Mkb = sb([128, 128], F32, "Mkb")
    identf = sb([128, 128], F32, "identf")
    ident = sb([128, 128], BF16, "ident")
    ones = sb([128, 1], F32, "ones")
    rgv = sb([128, 36], F32, "rgv", dma=True)
    rgc = sb([128, 16], F32, "rgc")
    wgate = sb([128, 8, 128], BF16, "wgate", dma=True)
    sm_ns = [sb([128, 2], F32, f"sm_n{i}") for i in range(3)]
    sm_h = sb([128, 8], F32, "sm_h")
    sm_rs = [sb([128, 8], F32, f"sm_r{i}") for i in range(3)]
    sm_f = sb([128, 4], F32, "sm_f")
    eB = sb([128, 4], F32, "eB")

    PS = P.es.enter_context(nc.psum_tensor("PS", [128, 8 * 512], F32))
    banks = [Buf(PS, f"bank{i}") for i in range(8)]
    bk = lambda i: PS[:, 512 * i:512 * (i + 1)]
    bkb = lambda i: PS[:, 512 * i:512 * (i + 1)].bitcast(BF16)
    rot = {"r": 0, "h": 0, "f": 0, "p": 0}
    FFN_BANKS = (4, 5, 6)
    PP_BANKS = (0, 1, 2, 3)

    def nbr():
        rot["r"] = (rot["r"] + 1) % 2
        return rot["r"]

    def nbh():
        rot["h"] = (rot["h"] + 1) % 2
        return 2 + rot["h"]

    def nbp():
        rot["p"] = (rot["p"] + 1) % len(PP_BANKS)
        return PP_BANKS[rot["p"]]

    def nbf():
        rot["f"] = (rot["f"] + 1) % len(FFN_BANKS)
        return FFN_BANKS[rot["f"]]

    PIN = 7
    op = P.op

    for kc in range(KC):
        for c0 in (0, 1536):
            P.dma_load("pool", w_in, I("dma_start", out=w_in[:, kc, c0:c0 + 1536], in_=win_d[kc * 128:(kc + 1) * 128, c0:c0 + 1536]))
    for kc in range(KC):
        P.dma_load("pool", w_out, I("dma_start", out=w_out[:, kc, :], in_=wout_d[kc * 128:(kc + 1) * 128, :]))
    P.dma_load("pool", wgate, I("dma_start", out=wgate[:, :, :].rearrange("p a b -> p (a b)"), in_=wgate_d[:, :]))
    P.dma_load("sp", gfin, I("dma_start", out=gfin[:, :], in_=gvec_d[2:3, :].partition_broadcast(128)))
    P.dma_load("sp", gcol, I("dma_start", out=gcol[:, :], in_=gcol_d[:, :]))
    P.dma_load("sp", lbc, I("dma_start", out=lbc[:, :], in_=hglb_d[0:1, :].partition_broadcast(128)))
    P.dma_load("sp", lb1, I("dma_start", out=lb1[:, 0:DHG], in_=hglb_d[1:2, :].partition_broadcast(128)))
    for i in range(4):
        P.dma_load("sp", ghg, I("dma_start", out=ghg[:, i * 128:(i + 1) * 128], in_=hgg_d[0:1, :].partition_broadcast(128)))
    P.dma_load("sp", rgv, I("dma_start", out=rgv[:, :], in_=rgv_d[:, :]))

    op("pool", [], [identf], I("memset", identf[:, :], 1.0))
    op("pool", [identf], [identf], I("affine_select", out=identf[:, :], in_=identf[:, :], pattern=[[-1, 128]], compare_op=ALU.is_equal, fill=0.0, base=0, channel_multiplier=1))
    op("pool", [], [Mqs], I("memset", Mqs[:, :], 1.0))
    op("pool", [Mqs], [Mqs], I("affine_select", out=Mqs[:, :], in_=Mqs[:, :], pattern=[[1, 128]], compare_op=ALU.is_ge, fill=0.0, base=0, channel_multiplier=-1))
    op("pool", [], [Mkd], I("memset", Mkd[:, :], 1.0))
    op("pool", [Mkd], [Mkd], I("affine_select", out=Mkd[:, :], in_=Mkd[:, :], pattern=[[-1, 128]], compare_op=ALU.is_gt, fill=0.0, base=0, channel_multiplier=1))
    op("pool", [], [ones], I("memset", ones[:, :], 1.0))
    op("pool", [], [S], I("memset", S[:, :], 0.0))
    op("pool", [], [Sb], I("memset", Sb[:, :], 0.0))
    op("pool", [], [hist], I("memset", hist[:, :, :], 0.0))
    op("pool", [], [hstate], I("memset", hstate[:, :], 0.0))
    op("pool", [], [Mkb], I("memset", Mkb[:, :], 0.0))
    op("dve", [identf], [ident], I("tensor_copy", out=ident[:, :], in_=identf[:, :]))
    op("dve", [Mqs], [Mka], I("tensor_scalar", out=Mka[:, :], in0=Mqs[:, :], scalar1=-1.0, scalar2=None, op0=ALU.mult))
    op("dve", [Mka], [Mka], I("memset", Mka[0:64, 64:128], 0.0))
    op("dve", [Mkd, Mkb], [Mkb], I("tensor_copy", out=Mkb[0:64, 0:64], in_=Mkd[0:64, 0:64]))
    op("dve", [Mka, Mkb], [Mkb], I("tensor_copy", out=Mkb[64:128, 64:128], in_=Mka[64:128, 64:128]))
    op("dve", [Mqs], [mask4], I("tensor_copy", out=mask4[:, :], in_=Mqs[:, :]))
    op("dve", [rgv], [rgc], I("tensor_scalar", out=rgc[:, 0:8], in0=rgv[:, 20:28], scalar1=0.5, scalar2=None, op0=ALU.mult))
    op("act", [rgv], [rgc], I("activation", out=rgc[:, 8:12], in_=rgv[:, 28:32], func=AF.Exp, scale=-1.0))
    op("act", [rgc], [rgc], I("activation", out=rgc[:, 8:12], in_=rgc[:, 8:12], func=AF.Ln, bias=1.0))
    op("dve", [rgc], [rgc], I("tensor_scalar", out=rgc[:, 12:16], in0=rgc[:, 8:12], scalar1=-4.0, scalar2=None, op0=ALU.mult))
    op("dve", [rgc], [rgc], I("tensor_scalar", out=rgc[:, 8:12], in0=rgc[:, 8:12], scalar1=-8.0, scalar2=None, op0=ALU.mult))
    op("dve", [lbc, lb1], [lbc], I("tensor_tensor", out=lbc[:, :], in0=lbc[:, :], in1=lb1[:, 0:DHG], op=ALU.subtract))
    op("act", [lbc], [lbc], I("activation", out=lbc[:, :], in_=lbc[:, :], func=AF.Tanh, scale=0.5))
    op("dve", [lbc], [lbc], I("tensor_scalar", out=lbc[:, :], in0=lbc[:, :], scalar1=-0.25, scalar2=0.25, op0=ALU.mult, op1=ALU.add))

    n_loads = len(groups) * NJ
    wgu_src = wgu_d.rearrange("(kc p) n -> p kc n", p=128)

    def emit_gu_load(st, L):
        if L >= n_loads:
            return
        j = L % NJ
        slot = gu_ring[L % NGU]
        wg, wu = slot["wg"], slot["wu"]
        st.dma_load("pool", wg, I("dma_start", out=wg[:, :, :], in_=wgu_src[:, :, j * 128:(j + 1) * 128]))
        st.dma_load("pool", wu, I("dma_start", out=wu[:, :, :], in_=wgu_src[:, :, DFF + j * 128:DFF + (j + 1) * 128]))

    def emit_wd_load(st, L):
        if L >= n_loads:
            return
        j = L % NJ
        wd = wd_ring[L % NWD]
        st.dma_load("pool", wd, I("dma_start", out=wd[:, :], in_=wdn_d[j * 128:(j + 1) * 128, :]))

    def rstd_from_ss(st, smx, ss_ap, out_ap, width):
        st.op("act", [smx], [smx], I("activation", out=out_ap, in_=ss_ap, func=AF.Ln, scale=1.0 / width, bias=EPS))
        st.op("act", [smx], [smx], I("activation", out=out_ap, in_=out_ap, func=AF.Exp, scale=-0.5))

    def norm_to_aT(st, hb, nt, c0, gi, aT, li):
        xn32, sm_n, b = xn32s[li], sm_ns[li], li
        st.op("act", [hb], [xn32, sm_n], I("activation", out=xn32[:nt, :], in_=hb[:nt, :], func=AF.Square, accum_out=sm_n[:nt, 0:1]))
        rstd_from_ss(st, sm_n, sm_n[:nt, 0:1], sm_n[:nt, 1:2], D)
        st.op("dve", [hb, sm_n], [xn32], I("tensor_scalar", out=xn32[:nt, :], in0=hb[:nt, :], scalar1=sm_n[:nt, 1:2], scalar2=None, op0=ALU.mult))
        for i in range(2):
            st.op("pe", [xn32, identf], [banks[b]], [I("transpose", out=bk(b)[:, j * 128:j * 128 + nt], in_=xn32[:nt, (4 * i + j) * 128:(4 * i + j + 1) * 128], identity=identf[:nt, :nt]) for j in range(4)])
            st.op("dve", [banks[b], gcol], [aT], I("tensor_tensor", out=aT[:, 4 * i:4 * i + 4, c0:c0 + nt], in0=bk(b).rearrange("p (a b) -> p a b", a=4)[:, :, 0:nt],
                                                    in1=gcol[:, gi * 8 + 4 * i:gi * 8 + 4 * i + 4].unsqueeze(2).to_broadcast([128, 4, nt]), op=ALU.mult))

    def mm_acc(out_ap, pairs):
        n = len(pairs)
        return [I("matmul", out_ap, lhsT=l, rhs=r, start=(i == 0), stop=(i == n - 1)) for i, (l, r) in enumerate(pairs)]

    def group_info(g):
        gtiles = [tiles[i] for i in groups[g]]
        gp0 = gtiles[0][0]
        N = sum(nt for _, nt in gtiles)
        chunks = [(0, N)] if N <= 512 else [(0, N // 2), (N // 2, N - N // 2)]
        return gtiles, gp0, N, chunks

    def gen_pre_tile(g, li, st):
        gtiles, gp0, N, chunks = group_info(g)
        hb = hsets[g % 2][li]
        aT = aTs[g % 2]
        p0, nt = gtiles[li]
        if p0 == 0:
            st.dma_load("sp", hb, I("dma_start", out=hb[0:NMETA, :], in_=meta_d[:, :]))
            st.dma_load("sp", hb, I("dma_start", out=hb[NMETA:128, :], in_=x_d[0:128 - NMETA, :]))
        else:
            st.dma_load("sp", hb, I("dma_start", out=hb[0:nt, :], in_=x_d[p0 - NMETA:p0 - NMETA + nt, :]))
        norm_to_aT(st, hb, nt, p0 - gp0, 0, aT, li)

    def gen_rg(g, st):
        gtiles, gp0, N, chunks = group_info(g)
        aT = aTs[g % 2]
        nbm = nbr
        F0, F1, F2, F3, F4, F5 = Rt
        H0 = RH0
        for blk in range(4):
            cw = [rgv[:, 4 * j + blk:4 * j + blk + 1] for j in range(4)]
            cb = rgv[:, 16 + blk:17 + blk]
            pX = [nbm() for _ in chunks]
            for ci, (cs, cn) in enumerate(chunks):
                col = blk * 128
                st.op("pe", [w_in, aT], [banks[pX[ci]]], mm_acc(bk(pX[ci])[:, 0:cn], [(w_in[:, kc, col:col + 128], aT[:, kc, cs:cs + cn]) for kc in range(KC)]))
            st.op("dve", [hist], [xpad], I("tensor_copy", out=xpad[:, 0:3], in_=hist[:, blk, :]))
            for ci, (cs, cn) in enumerate(chunks):
                st.op("act", [banks[pX[ci]]], [xpad], I("activation", out=xpad[:, 3 + cs:3 + cs + cn], in_=bk(pX[ci])[:, 0:cn], func=AF.Copy))
            pG = [nbm() for _ in chunks]
            for ci, (cs, cn) in enumerate(chunks):
                col = DRG + blk * 128
                st.op("pe", [w_in, aT], [banks[pG[ci]]], mm_acc(bk(pG[ci])[:, 0:cn], [(w_in[:, kc, col:col + 128], aT[:, kc, cs:cs + cn]) for kc in range(KC)]))
                st.op("act", [banks[pG[ci]]], [F5], I("activation", out=F5[:, cs:cs + cn], in_=bk(pG[ci])[:, 0:cn], func=AF.Copy))
            st.op("dve", [xpad, rgv], [F0], I("tensor_scalar", out=F0[:, 0:N], in0=xpad[:, 3:3 + N], scalar1=cw[3], scalar2=cb, op0=ALU.mult, op1=ALU.add))
            for j in (2, 1, 0):
                st.op("dve", [xpad, rgv, F0], [F0], I("scalar_tensor_tensor", out=F0[:, 0:N], in0=xpad[:, j:j + N], scalar=cw[j], in1=F0[:, 0:N], op0=ALU.mult, op1=ALU.add))
            st.op("dve", [xpad], [hist], I("tensor_copy", out=hist[:, blk, :], in_=xpad[:, N:N + 3]))
            st.op("act", [F0], [H0], I("activation", out=H0[:, 0:N], in_=F0[:, 0:N], func=AF.Copy))
            st.cut()
            for ci, (cs, cn) in enumerate(chunks):
                pR = nbm()
                st.op("pe", [wgate, H0], [banks[pR]], I("matmul", bk(pR)[:, 0:cn], lhsT=wgate[:, blk, :], rhs=H0[:, cs:cs + cn], start=True, stop=True))
                st.op("act", [banks[pR], rgc], [F1], I("activation", out=F1[:, cs:cs + cn], in_=bk(pR)[:, 0:cn], func=AF.Tanh, scale=0.5, bias=rgc[:, blk:blk + 1]))
                pI = nbm()
                st.op("pe", [wgate, H0], [banks[pI]], I("matmul", bk(pI)[:, 0:cn], lhsT=wgate[:, 4 + blk, :], rhs=H0[:, cs:cs + cn], start=True, stop=True))
                st.op("act", [banks[pI], rgc], [F2], I("activation", out=F2[:, cs:cs + cn], in_=bk(pI)[:, 0:cn], func=AF.Tanh, scale=0.5, bias=rgc[:, 4 + blk:5 + blk]))
            st.op("act", [F5], [F4], I("activation", out=F4[:, 0:N], in_=F5[:, 0:N], func=AF.Square))
            st.op("dve", [F4], [F4], I("tensor_scalar", out=F4[:, 0:N], in0=F4[:, 0:N], scalar1=0.044715, scalar2=1.0, op0=ALU.mult, op1=ALU.add))
            st.op("dve", [F4, F5], [F4], I("tensor_tensor", out=F4[:, 0:N], in0=F4[:, 0:N], in1=F5[:, 0:N], op=ALU.mult))
            st.op("act", [F4], [F4], I("activation", out=F4[:, 0:N], in_=F4[:, 0:N], func=AF.Tanh, scale=GELU_C))
            hcl = rgc[:, 12 + blk:13 + blk]
            cl = rgc[:, 8 + blk:9 + blk]
            st.op("act", [F1, rgc], [F3], I("activation", out=F3[:, 0:N], in_=F1[:, 0:N], func=AF.Exp, scale=hcl, bias=hcl))
            st.op("act", [F1, rgc], [F1], I("activation", out=F1[:, 0:N], in_=F1[:, 0:N], func=AF.Exp, scale=cl, bias=cl))
            st.op("act", [F1], [F1], I("activation", out=F1[:, 0:N], in_=F1[:, 0:N], func=AF.Ln, scale=-1.0, bias=1.0))
            st.op("act", [F1], [F1], I("activation", out=F1[:, 0:N], in_=F1[:, 0:N], func=AF.Exp, scale=0.5))
            st.op("dve", [F2, F0], [F2], I("scalar_tensor_tensor", out=F2[:, 0:N], in0=F2[:, 0:N], scalar=1.0, in1=F0[:, 0:N], op0=ALU.add, op1=ALU.mult))
            st.op("dve", [F2, F1], [F2], I("scalar_tensor_tensor", out=F2[:, 0:N], in0=F2[:, 0:N], scalar=0.5, in1=F1[:, 0:N], op0=ALU.mult, op1=ALU.mult))
            st.op("dve", [F3, F2, hstate], [F0], I("tensor_tensor_scan", out=F0[:, 0:N], data0=F3[:, 0:N], data1=F2[:, 0:N], initial=hstate[:, blk:blk + 1], op0=ALU.mult, op1=ALU.add))
            st.op("dve", [F0], [hstate], I("tensor_copy", out=hstate[:, blk:blk + 1], in_=F0[:, N - 1:N]))
            st.op("dve", [F4, F5], [F4], I("scalar_tensor_tensor", out=F4[:, 0:N], in0=F4[:, 0:N], scalar=1.0, in1=F5[:, 0:N], op0=ALU.add, op1=ALU.mult))
            st.op("dve", [F4, F0], [F4], I("scalar_tensor_tensor", out=F4[:, 0:N], in0=F4[:, 0:N], scalar=0.5, in1=F0[:, 0:N], op0=ALU.mult, op1=ALU.mult))
            st.op("dve", [F4, rgv], [yT], I("tensor_scalar", out=yT[:, blk, 0:N], in0=F4[:, 0:N], scalar1=rgv[:, 32 + blk:33 + blk], scalar2=None, op0=ALU.mult))
            st.op("act", [F4], [F2], I("activation", out=F2[:, 0:N], in_=F4[:, 0:N], func=AF.Square))
            st.cut()
            st.op("pe", [F2, ones], [banks[PIN]], [I("matmul", bk(PIN)[:nt, li * 4 + blk:li * 4 + blk + 1], lhsT=F2[:, p0 - gp0:p0 - gp0 + nt], rhs=ones[:, 0:1], start=True, stop=True) for li, (p0, nt) in enumerate(gtiles)])

    def gen_hg(g, st):
        gtiles, gp0, N, chunks = group_info(g)
        aT = aTs[g % 2]
        nbm = nbh
        for li, (p0, nt) in enumerate(gtiles):
            c0 = p0 - gp0
            fK, fLF, fQ, fG, fE0, fE1 = Ft
            hV, hKA, hQB, hKB, hQS, hKD, hAT = Ht
            st.cut()
            pf = nbm()
            st.op("pe", [aT, w_in], [banks[pf]], mm_acc(bk(pf)[:nt, :], [(aT[:, kc, c0:c0 + nt], w_in[:, kc, 1536:2048]) for kc in range(KC)]))
            st.op("act", [banks[pf]], [fK], I("activation", out=fK[:nt, 0:512], in_=bk(pf)[:nt, :], func=AF.Tanh, scale=0.5))
            pq = nbm()
            st.op("pe", [aT, w_in], [banks[pq]], mm_acc(bk(pq)[:nt, :], [(aT[:, kc, c0:c0 + nt], w_in[:, kc, 1024:1536]) for kc in range(KC)]))
            st.op("act", [banks[pq]], [fQ], I("activation", out=fQ[:nt, 0:512], in_=bk(pq)[:nt, :], func=AF.Silu))
            pg = nbm()
            st.op("pe", [aT, w_in], [banks[pg]], mm_acc(bk(pg)[:nt, :], [(aT[:, kc, c0:c0 + nt], w_in[:, kc, 2560:3072]) for kc in range(KC)]))
            st.op("act", [banks[pg]], [fG], I("activation", out=fG[:nt, 0:512], in_=bk(pg)[:nt, :], func=AF.Silu))
            pi_ = nbm()
            st.op("pe", [aT, w_in], [banks[pi_]], mm_acc(bk(pi_)[:nt, :], [(aT[:, kc, c0:c0 + nt], w_in[:, kc, 2048:2560]) for kc in range(KC)]))
            st.op("act", [banks[pi_]], [hV], I("activation", out=hV[:nt, 0:512], in_=bk(pi_)[:nt, :], func=AF.Copy))
            st.op("dve", [fK], [fK], I("tensor_scalar", out=fK[:nt, 0:512], in0=fK[:nt, 0:512], scalar1=-1.0, scalar2=1.0, op0=ALU.mult, op1=ALU.add))
            st.op("dve", [fK, lbc], [fK], I("tensor_tensor", out=fK[:nt, 0:512], in0=fK[:nt, 0:512], in1=lbc[:nt, :], op=ALU.mult))
            st.op("act", [fK], [fLF], I("activation", out=fLF[:nt, 0:512], in_=fK[:nt, 0:512], func=AF.Ln, scale=-1.0, bias=1.0))
            st.op("dve", [fG, ghg], [fG], I("tensor_tensor", out=fG[:nt, 0:512], in0=fG[:nt, 0:512], in1=ghg[:nt, :], op=ALU.mult))
            st.cut()
            st.op("pe", [fLF, ones], [banks[PIN]], [I("matmul", bk(PIN)[:, 32 + hh:33 + hh], lhsT=fLF[:nt, hh * 128:(hh + 1) * 128], rhs=ones[:nt, 0:1], start=True, stop=True) for hh in range(4)])
            st.op("act", [banks[PIN]], [eB], I("activation", out=eB[:, 0:4], in_=bk(PIN)[:, 32:36], func=AF.Exp))

            def expmul(M, scales_srcs_dsts):
                b = nbm()
                st.op("pe", [M, fLF], [banks[b]], I("matmul", bk(b)[:nt, :], lhsT=M[:nt, :nt], rhs=fLF[:nt, 0:512], start=True, stop=True))
                for scale, src, dst, tmp in scales_srcs_dsts:
                    st.op("act", [banks[b]], [tmp], I("activation", out=tmp[:nt, 0:512], in_=bk(b)[:nt, :], func=AF.Exp, scale=scale))
                    st.op("dve", [tmp, src], [dst], I("tensor_tensor", out=dst[:nt, 0:512], in0=src[:nt, 0:512], in1=tmp[:nt, 0:512], op=ALU.mult))
            expmul(Mka, [(1.0, fK, hKA, fE0), (-1.0, fQ, hQB, fE1)])
            expmul(Mkb, [(1.0, fK, hKB, fE0)])
            expmul(Mqs, [(1.0, fQ, hQS, fE1)])
            expmul(Mkd, [(1.0, fK, hKD, fE0)])
            st.cut()
            for half, srcs in ((0, (hQB, hQS)), (1, (hKA, hKB))):
                b = nbm()
                st.op("pe", [srcs[0], srcs[1], ident], [banks[b]], [I("transpose", out=bkb(b)[:, (vi * 4 + hh) * 128:(vi * 4 + hh) * 128 + nt], in_=s_[:nt, hh * 128:(hh + 1) * 128], identity=ident[:nt, :nt]) for vi, s_ in enumerate(srcs) for hh in range(4)])
                dst = trT[:, half * 8:half * 8 + 8, 0:nt]
                src = bkb(b).rearrange("p (a b) -> p a b", a=8)[:, :, 0:nt]
                if half == 0:
                    st.op("act", [banks[b]], [trT], I("activation", out=dst, in_=src, func=AF.Copy))
                else:
                    st.op("dve", [banks[b]], [trT], I("tensor_copy", out=dst, in_=src))
            st.cut()
            pA = nbm()
            n0 = min(64, nt)
            insA = []
            for hh in range(4):
                insA.append(I("matmul", bk(pA)[:nt, hh * 128:hh * 128 + n0], lhsT=trT[:, 8 + hh, 0:nt], rhs=trT[:, 0 + hh, 0:n0], start=True, stop=True))
                if nt > 64:
                    insA.append(I("matmul", bk(pA)[:nt, hh * 128 + 64:hh * 128 + nt], lhsT=trT[:, 12 + hh, 0:nt], rhs=trT[:, 0 + hh, 64:nt], start=True, stop=True))
            st.op("pe", [trT], [banks[pA]], insA)
            pU = nbm()
            st.op("pe", [hKD, hV], [banks[pU]], [I("matmul", bk(pU)[:, hh * 128:(hh + 1) * 128], lhsT=hKD[:nt, hh * 128:(hh + 1) * 128], rhs=hV[:nt, hh * 128:(hh + 1) * 128], start=True, stop=True) for hh in range(4)])
            if nt == 128:
                st.op("dve", [banks[pA], mask4], [hAT], I("tensor_tensor", out=hAT[:nt, 0:512].rearrange("p (a b) -> p a b", a=4), in0=bk(pA)[:nt, :].rearrange("p (a b) -> p a b", a=4), in1=mask4[:nt, :].unsqueeze(1).to_broadcast([nt, 4, 128]), op=ALU.mult))
            else:
                for hh in range(4):
                    st.op("dve", [banks[pA], mask4], [hAT], I("tensor_tensor", out=hAT[:nt, hh * 128:hh * 128 + nt], in0=bk(pA)[:nt, hh * 128:hh * 128 + nt], in1=mask4[:nt, 0:nt], op=ALU.mult))
            st.cut()
            pO = nbm()
            insO = []
            for hh in range(4):
                hs_ = slice(hh * 128, (hh + 1) * 128)
                insO.append(I("matmul", bk(pO)[:nt, hs_], lhsT=hAT[:nt, hh * 128:hh * 128 + nt], rhs=hV[:nt, hs_], start=True, stop=False))
                insO.append(I("matmul", bk(pO)[:nt, hs_], lhsT=trT[:, 4 + hh, 0:nt], rhs=Sb[:, hs_], start=False, stop=True))
            st.op("pe", [hAT, hV, trT, Sb], [banks[pO]], insO)
            for hh in range(4):
                hs_ = slice(hh * 128, (hh + 1) * 128)
                st.op("dve", [S, eB, banks[pU]], [S], I("scalar_tensor_tensor", out=S[:, hs_], in0=S[:, hs_], scalar=eB[:, hh:hh + 1], in1=bk(pU)[:, hs_], op0=ALU.mult, op1=ALU.add))
            st.op("act", [S], [Sb], I("activation", out=Sb[:, :], in_=S[:, :], func=AF.Copy))
            for hh in range(4):
                hs_ = slice(hh * 128, (hh + 1) * 128)
                st.op("act", [banks[pO]], [fE0, sm_h], I("activation", out=fE0[:nt, hs_], in_=bk(pO)[:nt, hs_], func=AF.Square, accum_out=sm_h[:nt, hh:hh + 1]))
            rstd_from_ss(st, sm_h, sm_h[:nt, 0:4], sm_h[:nt, 4:8], 128)
            st.op("dve", [banks[pO], sm_h], [fE1], I("tensor_tensor", out=fE1[:nt, 0:512].rearrange("p (a b) -> p a b", a=4), in0=bk(pO)[:nt, :].rearrange("p (a b) -> p a b", a=4), in1=sm_h[:nt, 4:8].unsqueeze(2).to_broadcast([nt, 4, 128]), op=ALU.mult))
            st.op("dve", [fE1, fG], [hKA], I("tensor_tensor", out=hKA[:nt, 0:512], in0=fE1[:nt, 0:512], in1=fG[:nt, 0:512], op=ALU.mult))
            st.cut()
            b = nbm()
            st.op("pe", [hKA, ident], [banks[b]], [I("transpose", out=bkb(b)[:, hh * 128:hh * 128 + nt], in_=hKA[:nt, hh * 128:(hh + 1) * 128], identity=ident[:nt, :nt]) for hh in range(4)])
            st.op("act", [banks[b]], [yT], I("activation", out=yT[:, 4:8, c0:c0 + nt], in_=bkb(b)[:, 0:512].rearrange("p (a b) -> p a b", a=4)[:, :, 0:nt], func=AF.Copy))

    def gen_post_tile(g, li, st):
        gtiles, gp0, N, chunks = group_info(g)
        hb = hsets[g % 2][li]
        aT = aTs[g % 2]
        sm_r = sm_rs[li]
        p0, nt = gtiles[li]
        c0 = p0 - gp0
        b = li
        st.op("dve", [banks[PIN]], [sm_r], I("tensor_copy", out=sm_r[:nt, 0:4], in_=bk(PIN)[:nt, li * 4:li * 4 + 4]))
        st.op("dve", [sm_r], [sm_r], I("tensor_tensor", out=sm_r[:nt, 4:6], in0=sm_r[:nt, 0:2], in1=sm_r[:nt, 2:4], op=ALU.add))
        st.op("dve", [sm_r], [sm_r], I("tensor_tensor", out=sm_r[:nt, 6:7], in0=sm_r[:nt, 4:5], in1=sm_r[:nt, 5:6], op=ALU.add))
        rstd_from_ss(st, sm_r, sm_r[:nt, 6:7], sm_r[:nt, 7:8], DRG)
        for half in range(2):
            cs_ = slice(half * 512, (half + 1) * 512)
            st.op("pe", [yT, w_out], [banks[b]], mm_acc(bk(b)[:nt, :], [(yT[:, blk, c0:c0 + nt], w_out[:, blk, cs_]) for blk in range(4)]))
            st.op("dve", [banks[b], sm_r, hb], [hb], I("scalar_tensor_tensor", out=hb[:nt, cs_], in0=bk(b)[:nt, :], scalar=sm_r[:nt, 7:8], in1=hb[:nt, cs_], op0=ALU.mult, op1=ALU.add))
            st.op("pe", [yT, w_out], [banks[b]], mm_acc(bk(b)[:nt, :], [(yT[:, blk, c0:c0 + nt], w_out[:, blk, cs_]) for blk in range(4, 8)]))
            st.op("dve", [banks[b], hb], [hb], I("tensor_tensor", out=hb[:nt, cs_], in0=bk(b)[:nt, :], in1=hb[:nt, cs_], op=ALU.add))
        norm_to_aT(st, hb, nt, c0, 1, aT, li)

    def gen_ffn(g, st):
        gtiles, gp0, N, chunks = group_info(g)
        hbuf = hsets[g % 2]
        aT = aTs[g % 2]
        j0 = 0
        sbi = 0
        while j0 < NJ:
            nj = min(SBJ, NJ - j0)
            ab = actb[sbi % 2]
            for jj in range(nj):
                L = g * NJ + j0 + jj
                slot = gu_ring[L % NGU]
                for ci, (cs, cn) in enumerate(chunks):
                    pg_, pu_ = nbf(), nbf()
                    st.op("pe", [slot["wg"], aT], [banks[pg_]], mm_acc(bk(pg_)[:, 0:cn], [(slot["wg"][:, kc, :], aT[:, kc, cs:cs + cn]) for kc in range(KC)]))
                    st.op("pe", [slot["wu"], aT], [banks[pu_]], mm_acc(bk(pu_)[:, 0:cn], [(slot["wu"][:, kc, :], aT[:, kc, cs:cs + cn]) for kc in range(KC)]))
                    tmp = Gt[jj % 2]
                    st.op("act", [banks[pg_]], [tmp], I("activation", out=tmp[:, cs:cs + cn], in_=bk(pg_)[:, 0:cn], func=AF.Silu))
                    st.op("dve", [tmp, banks[pu_]], [ab], I("tensor_tensor", out=ab[:, jj, cs:cs + cn], in0=tmp[:, cs:cs + cn], in1=bk(pu_)[:, 0:cn], op=ALU.mult))
                emit_gu_load(st, L + NGU)
            wds = [wd_ring[(g * NJ + j0 + jj) % NWD] for jj in range(nj)]
            for li, (p0, nt) in enumerate(gtiles):
                c0 = p0 - gp0
                hb = hbuf[li]
                for half in range(2):
                    pd = nbf()
                    cs_ = slice(half * 512, (half + 1) * 512)
                    st.op("pe", [ab] + wds, [banks[pd]], mm_acc(bk(pd)[:nt, :], [(ab[:, jj, c0:c0 + nt], wds[jj][:, cs_]) for jj in range(nj)]))
                    st.op("dve", [banks[pd], hb], [hb], I("tensor_tensor", out=hb[:nt, cs_], in0=bk(pd)[:nt, :], in1=hb[:nt, cs_], op=ALU.add))
            for jj in range(nj):
                emit_wd_load(st, g * NJ + j0 + jj + NWD)
            j0 += nj
            sbi += 1
        for li, (p0, nt) in enumerate(gtiles):
            hb = hbuf[li]
            for half in range(2):
                jb = nbf()
                st.op("act", [hb], [banks[jb], sm_f], I("activation", out=bk(jb)[:nt, :], in_=hb[:nt, half * 512:(half + 1) * 512], func=AF.Square, accum_out=sm_f[:nt, 2 + half:3 + half]))
            st.op("dve", [sm_f], [sm_f], I("tensor_tensor", out=sm_f[:nt, 0:1], in0=sm_f[:nt, 2:3], in1=sm_f[:nt, 3:4], op=ALU.add))
            rstd_from_ss(st, sm_f, sm_f[:nt, 0:1], sm_f[:nt, 1:2], D)
            st.op("dve", [hb, sm_f, gfin], [hb], I("scalar_tensor_tensor", out=hb[:nt, :], in0=hb[:nt, :], scalar=sm_f[:nt, 1:2], in1=gfin[:nt, :], op0=ALU.mult, op1=ALU.mult))
            if p0 == 0:
                st.dma_store("sp", hb, I("dma_start", out=out_d[0:128 - NMETA, :], in_=hb[NMETA:128, :]))
            else:
                st.dma_store("sp", hb, I("dma_start", out=out_d[p0 - NMETA:p0 - NMETA + nt, :], in_=hb[0:nt, :]))
            st.cut()

    def flat(segs):
        return [it for seg in segs for it in seg]

    S_SET = (AF.Silu, AF.Tanh)
    L_SET = (AF.Exp, AF.Ln)

    def act_set(item):
        kind, a_ = item
        if kind != "op" or a_[0] != "act":
            return None
        insts = a_[3]
        if isinstance(insts, tuple):
            insts = [insts]
        f = insts[-1][2].get("func")
        if f in S_SET:
            return "S"
        if f in L_SET:
            return "L"
        return None

    ZIP_TOL = 0.03

    def zip_n(lists, tol=None):
        tol = ZIP_TOL if tol is None else tol
        lists = [l for l in lists if l]
        out = []
        idx = [0] * len(lists)
        cur = None
        while True:
            live = [k for k in range(len(lists)) if idx[k] < len(lists[k])]
            if not live:
                break
            frac = {k: (idx[k] + 1) / len(lists[k]) for k in live}
            fmin = min(frac.values())
            cands = [k for k in live if frac[k] <= fmin + tol]

            def cost(k):
                st_ = act_set(lists[k][idx[k]])
                return (1 if (st_ is not None and cur is not None and st_ != cur) else 0, frac[k])
            k = min(cands, key=cost)
            it = lists[k][idx[k]]
            st_ = act_set(it)
            if st_ is not None:
                cur = st_
            out.append(it)
            idx[k] += 1
        return out

    def zip_items(A, B):
        return zip_n([A, B])

    def mixer_items(g):
        if PROBE_SKIP_MIXER:
            return []
        ntile = len(groups[g])
        pres, posts = [], []
        for li in range(ntile):
            a_, b_ = Stream(), Stream()
            gen_pre_tile(g, li, a_)
            gen_post_tile(g, li, b_)
            pres.append(flat(a_.segs))
            posts.append(flat(b_.segs))
        rg, hg = Stream(), Stream()
        gen_rg(g, rg)
        gen_hg(g, hg)
        return zip_n(pres) + zip_items(flat(rg.segs), flat(hg.segs)) + zip_n(posts)

    NG = len(groups)
    st0 = Stream()
    for L0 in range(NGU):
        emit_gu_load(st0, L0)
    for L0 in range(NWD):
        emit_wd_load(st0, L0)
    flush_segment(P, flat(st0.segs) + mixer_items(0))
    for g in range(NG):
        sf = Stream()
        if not PROBE_SKIP_FFN:
            gen_ffn(g, sf)
        mit = mixer_items(g + 1) if g + 1 < NG else []
        flush_segment(P, zip_items(flat(sf.segs), mit))
    P.final_wait("sp", hsets[0] + hsets[1])
    P.emit()
    P.close()
    return nc


def _layout_inputs(inp):
    f = lambda a: np.ascontiguousarray(np.asarray(a, dtype=np.float32))
    cw = f(inp["conv_w"])[0]
    cols = []
    for j in range(4):
        cols.append(cw[j].reshape(4, 128).T)
    for nm in ("conv_b", "b_rgate", "b_igate", "lru_lambda", "rg_norm_g"):
        cols.append(f(inp[nm])[0].reshape(4, 128).T)
    rgv = f(np.concatenate(cols, axis=1))
    wg = np.zeros((128, 8, 128), np.float32)
    for gi, nm in enumerate(("w_rgate", "w_igate")):
        w = f(inp[nm])[0]
        for blk in range(4):
            wg[0:64, gi * 4 + blk, 0:64] = w[2 * blk]
            wg[64:128, gi * 4 + blk, 64:128] = w[2 * blk + 1]
    shared = {
        "meta": f(inp["meta_tokens"]),
        "w_in": f(inp["w_in"])[0],
        "w_out": f(inp["w_out"])[0],
        "w_gu": f(inp["w_gate_up"])[0],
        "w_down": f(inp["w_down"])[0],
        "gvec": f(np.stack([f(inp["mix_norm_g"])[0], f(inp["ffn_norm_g"])[0], f(inp["final_norm_g"])], 0)),
        "rgv": rgv,
        "gcol": f(np.concatenate([f(inp["mix_norm_g"])[0].reshape(8, 128).T, f(inp["ffn_norm_g"])[0].reshape(8, 128).T], axis=1)),
        "wgate": f(wg.reshape(128, 8 * 128)),
        "hglb": f(inp["hg_lower_bound"]),
        "hgg": f(inp["hg_norm_g"]),
    }
    x = f(inp["x"])
    return [dict(shared, x=x[c]) for c in range(NCORES)]


def kernel(**inputs):
    in_maps = _layout_inputs(inputs)
    nc = build_nc()
    res = run_bass_kernel_spmd(nc, in_maps, core_ids=list(range(NCORES)))
    return np.stack([np.asarray(r["out"], dtype=np.float32) for r in res.results], axis=0)
```
